# Optimizing a Trainium2 kernel written in Bass

```python
import math
import jax, jax.numpy as jnp
from jax import lax
import numpy as np

D_MODEL = 1024
BATCH = 2
SEQ = 8192
DEPTH = 2

CHUNK = 64
N_LEFT_CHUNKS = 8
BAND_CHUNKS = N_LEFT_CHUNKS + 1
N_HEADS = 16
HEAD_DIM = D_MODEL // N_HEADS
MAX_REL = 2 * CHUNK
N_REL = 2 * MAX_REL + 1
CONV_WIDTH = 3
D_FF = ((8 * D_MODEL // 3 + 255) // 256) * 256
N_A = DEPTH // 2
N_B = DEPTH - N_A
EPS = 1e-6

kernel_name = "yoco_shortconv_chunkattn_sandwich_adaln"


def rms_norm(x, g):
    xf = x.astype(jnp.float32)
    y = xf * lax.rsqrt(jnp.mean(xf * xf, axis=-1, keepdims=True) + EPS)
    return (y * g.astype(jnp.float32)).astype(x.dtype)


def modulate(h, shift, scale):
    return h * (1.0 + scale[:, None, :]) + shift[:, None, :]


def short_conv_mixer(h, w_in, conv_k, w_out):
    S = h.shape[1]
    bcx = h @ w_in
    b_gate, c_gate, xin = jnp.split(bcx, 3, axis=-1)
    z = c_gate * xin
    zp = jnp.pad(z, ((0, 0), (CONV_WIDTH - 1, 0), (0, 0)))
    conv = sum(conv_k[k] * zp[:, k:k + S] for k in range(CONV_WIDTH))
    return (b_gate * conv) @ w_out


def gather_band(t):
    Bsz, S = t.shape[0], t.shape[1]
    nc = S // CHUNK
    tc = t.reshape(Bsz, nc, CHUNK, N_HEADS, HEAD_DIM)
    tp = jnp.pad(tc, ((0, 0), (N_LEFT_CHUNKS, 0), (0, 0), (0, 0), (0, 0)))
    idx = jnp.arange(nc)[:, None] + jnp.arange(BAND_CHUNKS)[None, :]
    band = tp[:, idx]
    return band.reshape(Bsz, nc, BAND_CHUNKS * CHUNK, N_HEADS, HEAD_DIM)


def chunk_band_attention(h, k_band, v_band, w_q, w_o, rel_bias):
    Bsz, S, _ = h.shape
    nc = S // CHUNK
    q = (h @ w_q).reshape(Bsz, nc, CHUNK, N_HEADS, HEAD_DIM)
    scores = jnp.einsum('bnqhd,bnkhd->bhnqk', q, k_band).astype(jnp.float32)
    scores = scores * (HEAD_DIM ** -0.5)
    a = jnp.arange(CHUNK)[:, None]
    kk = jnp.arange(BAND_CHUNKS * CHUNK)[None, :]
    j, b = kk // CHUNK, kk % CHUNK
    rel = (N_LEFT_CHUNKS - j) * CHUNK + a - b
    rel_idx = jnp.clip(rel, -MAX_REL, MAX_REL) + MAX_REL
    bias = rel_bias.astype(jnp.float32)[:, rel_idx]
    scores = scores + bias[None, :, None]
    key_chunk = jnp.arange(nc)[:, None] + (jnp.arange(BAND_CHUNKS * CHUNK)[None, :] // CHUNK) - N_LEFT_CHUNKS
    valid = key_chunk >= 0
    scores = jnp.where(valid[None, None, :, None, :], scores, jnp.finfo(jnp.float32).min)
    p = jax.nn.softmax(scores, axis=-1).astype(v_band.dtype)
    o = jnp.einsum('bhnqk,bnkhd->bnqhd', p, v_band)
    return o.reshape(Bsz, S, D_MODEL) @ w_o


def swiglu(h, w_in, w_out):
    gu = h @ w_in
    g, u = jnp.split(gu, 2, axis=-1)
    return (jax.nn.silu(g) * u) @ w_out


def setup_inputs(seed: int = 0) -> dict:
    key = jax.random.key(seed)
    ks = jax.random.split(key, 20)
    nrm = lambda k, shape, fan: jax.random.normal(k, shape, jnp.float32) * (fan ** -0.5)
    D = D_MODEL
    return {
        "x": jax.random.normal(ks[0], (BATCH, SEQ, D), jnp.float32),
        "c": jax.random.normal(ks[1], (BATCH, D), jnp.float32),
        "mod_w": nrm(ks[2], (DEPTH, D, 6 * D), D) * 0.3,
        "mod_b": 0.05 * jax.random.normal(ks[3], (DEPTH, 6 * D), jnp.float32),
        "norm_g": 1.0 + 0.05 * jax.random.normal(ks[4], (DEPTH, 4, D), jnp.float32),
        "ffn_w_in": nrm(ks[5], (DEPTH, D, 2 * D_FF), D),
        "ffn_w_out": nrm(ks[6], (DEPTH, D_FF, D), D_FF),
        "conv_w_in": nrm(ks[7], (N_A, D, 3 * D), D),
        "conv_k": nrm(ks[8], (N_A, CONV_WIDTH, D), CONV_WIDTH),
        "conv_w_out": nrm(ks[9], (N_A, D, D), D),
        "kv_mod_w": nrm(ks[10], (D, 2 * D), D) * 0.3,
        "kv_mod_b": 0.05 * jax.random.normal(ks[11], (2 * D,), jnp.float32),
        "kv_norm_g": 1.0 + 0.05 * jax.random.normal(ks[12], (D,), jnp.float32),
        "w_kv": nrm(ks[13], (D, 2 * D), D),
        "attn_w_q": nrm(ks[14], (N_B, D, D), D),
        "attn_w_o": nrm(ks[15], (N_B, D, D), D),
        "rel_bias": 0.5 * jax.random.normal(ks[16], (N_B, N_HEADS, N_REL), jnp.float32),
    }


def reference(x, c, mod_w, mod_b, norm_g, ffn_w_in, ffn_w_out, conv_w_in, conv_k,
              conv_w_out, kv_mod_w, kv_mod_b, kv_norm_g, w_kv, attn_w_q, attn_w_o,
              rel_bias):
    Bsz, S, _ = x.shape
    silu_c = jax.nn.silu(c)
    k_band = None
    v_band = None
    for layer in range(DEPTH):
        mod = silu_c @ mod_w[layer] + mod_b[layer]
        sh1, sc1, g1, sh2, sc2, g2 = jnp.split(mod, 6, axis=-1)
        h = modulate(rms_norm(x, norm_g[layer, 0]), sh1, sc1)
        if layer < N_A:
            y = short_conv_mixer(h, conv_w_in[layer], conv_k[layer], conv_w_out[layer])
        else:
            if layer == N_A:
                kv_sh, kv_sc = jnp.split(silu_c @ kv_mod_w + kv_mod_b, 2, axis=-1)
                hkv = modulate(rms_norm(x, kv_norm_g), kv_sh, kv_sc)
                k, v = jnp.split(hkv @ w_kv, 2, axis=-1)
                k_band = gather_band(k.reshape(Bsz, S, N_HEADS, HEAD_DIM))
                v_band = gather_band(v.reshape(Bsz, S, N_HEADS, HEAD_DIM))
            bi = layer - N_A
            y = chunk_band_attention(h, k_band, v_band, attn_w_q[bi], attn_w_o[bi], rel_bias[bi])
        x = x + g1[:, None, :] * rms_norm(y, norm_g[layer, 1])
        h = modulate(rms_norm(x, norm_g[layer, 2]), sh2, sc2)
        y = swiglu(h, ffn_w_in[layer], ffn_w_out[layer])
        x = x + g2[:, None, :] * rms_norm(y, norm_g[layer, 3])
    return x
```

```python
import numpy as np
from contextlib import ExitStack
import concourse.bass as bass
import concourse.mybir as mybir
from concourse.bass_utils import run_bass_kernel_spmd

F32 = mybir.dt.float32
BF16 = mybir.dt.bfloat16
AF = mybir.ActivationFunctionType
ALU = mybir.AluOpType

D = 1024
KC = 8
HT = 512
NH = 512 // HT
NST = HT // 128
NKS = 1024 // HT
NHALF = 2560 // HT
NHALO = 512 // HT
DFF = 2816
HC = 22
NSLOT = 12
EPS = 1e-6
NEG = -30000.0
N_CORES = 8
OWN = 2048
HALO = 512

V_MODB = (0, 48)
V_NG = 96
V_CK = 160
V_KVB = 184
V_KVG = 200
V_C = 208
NV = 216

ENGS = ("pe", "act", "dve", "pool", "sp")


class Op:
    __slots__ = ("eng", "fn", "reads", "writes", "dma", "deps", "signal", "sigval", "idx", "chan", "label")

    def __init__(self, eng, fn, reads, writes, dma, chan):
        self.eng = eng
        self.fn = fn
        self.reads = reads
        self.writes = writes
        self.dma = dma
        self.chan = chan
        self.deps = []
        self.signal = False
        self.sigval = 0
        self.idx = -1


class Sched:
    def __init__(self):
        self.ops = []
        self.dry = False
        self.label = "pro"

    def add(self, eng, fn, reads=(), writes=(), dma=False, chan=None):
        if self.dry:
            return None
        op = Op(eng, fn, tuple(reads), tuple(writes), dma, chan)
        op.idx = len(self.ops)
        op.label = self.label
        self.ops.append(op)
        return op

    def analyze(self):
        last_w = {}
        readers = {}
        for op in self.ops:
            deps = {}
            for u in op.reads:
                w = last_w.get(u)
                if w is not None:
                    deps[w.idx] = (w, "raw")
            for u in op.writes:
                w = last_w.get(u)
                if w is not None and w.idx not in deps:
                    deps[w.idx] = (w, "waw")
                for r in readers.get(u, ()):
                    if r.idx not in deps:
                        deps[r.idx] = (r, "war")
            for u in op.reads:
                lst = readers.setdefault(u, [])
                if not op.dma:
                    lst[:] = [r for r in lst if r.dma or r.eng != op.eng]
                lst.append(op)
            for u in op.writes:
                last_w[u] = op
                readers[u] = []
            best = {}
            for d, kind in deps.values():
                if d is op:
                    continue
                if d.dma:
                    op.deps.append(d)
                    d.signal = True
                    continue
                if d.eng == op.eng and not op.dma:
                    if op.eng == "pe":
                        continue
                    if kind == "war":
                        continue
                b = best.get(d.eng)
                if b is None or d.idx > b.idx:
                    best[d.eng] = d
            for d in best.values():
                op.deps.append(d)
                d.signal = True
        cnt = {}
        for op in self.ops:
            if op.dma:
                op.signal = True
            if not op.signal:
                continue
            key = ("chan", op.chan) if op.dma else ("eng", op.eng)
            inc = 16 if op.dma else 1
            cnt[key] = cnt.get(key, 0) + inc
            op.sigval = cnt[key]
        self.sem_keys = list(cnt.keys())
        return cnt

    def emit(self, block, sems):
        per_eng = {e: [] for e in ENGS}
        for op in self.ops:
            per_eng[op.eng].append(op)

        def run(eng_name):
            def body(eng):
                waited = {}
                for op in per_eng[eng_name]:
                    need = {}
                    for d in op.deps:
                        key = ("chan", d.chan) if d.dma else ("eng", d.eng)
                        if d.sigval > need.get(key, 0):
                            need[key] = d.sigval
                    for key, v in need.items():
                        if waited.get(key, 0) >= v:
                            continue
                        eng.wait_ge(sems[key], v)
                        waited[key] = v
                    ins = op.fn(eng)
                    if op.signal:
                        key = ("chan", op.chan) if op.dma else ("eng", op.eng)
                        ins.then_inc(sems[key], 16 if op.dma else 1)
            return body

        block.tensor(run("pe"))
        block.scalar(run("act"))
        block.vector(run("dve"))
        block.gpsimd(run("pool"))
        block.sync(run("sp"))


class Blk:
    __slots__ = ("i", "slot")

    def __init__(self, i, slot):
        self.i = i
        self.slot = slot


class WRing:
    def __init__(self, S, wring, plan=None):
        self.S = S
        self.wring = wring
        self.record = plan is None
        self.plan = [] if plan is None else plan
        self.i_use = 0
        self.i_issue = 0
        self.done_flags = {}

    def _issue_ready(self):
        while self.i_issue < len(self.plan):
            i = self.i_issue
            if i >= NSLOT and not self.done_flags.get(i - NSLOT, False):
                break
            src, nkc, ncols = self.plan[i]
            slot = i % NSLOT
            wr = self.wring
            self.S.add("pool", lambda e, src=src, nkc=nkc, ncols=ncols, slot=slot: e.dma_start(
                out=wr[:, slot, 0:nkc, 0:ncols], in_=src), reads=[], writes=[("w", slot)], dma=True, chan=("w", slot))
            self.i_issue += 1

    def start(self):
        if not self.record:
            self._issue_ready()

    def take(self, wd, r0, nkc, c0, ncols):
        i = self.i_use
        self.i_use += 1
        if self.record:
            src = wd[r0 * 128:(r0 + nkc) * 128, c0:c0 + ncols].rearrange("(kc p) n -> p kc n", p=128)
            self.plan.append((src, nkc, ncols))
        else:
            assert i < self.i_issue, ("weight ring overflow: block taken before its DMA could be issued", i, self.i_issue)
        return Blk(i, i % NSLOT)

    def done(self, blk):
        if self.record:
            return
        self.done_flags[blk.i] = True
        self._issue_ready()


def build_nc(do_l1=True, skew=True, nhalf=NHALF):
    nc = bass.Bass("TRN2", target_bir_lowering=False)

    def din(name, shape):
        return nc.dram_tensor(name, list(shape), F32, kind="ExternalInput").ap()

    x_d = din("x", [HALO + OWN, D])
    xpre_d = din("xpre", [128, 16])
    vecs_d = din("vecs", [128, NV])
    cb_d = din("cb", [128, 16])
    hv_d = din("hv", [128, 1])
    bt_d = din("btiles", [128, 4096])
    ident_d = din("ident", [128, 128])
    modw_d = [din("mod_w0", [D, 6 * D]), din("mod_w1", [D, 6 * D])]
    ffi_d = [din("ffn_in0", [D, 2 * DFF]), din("ffn_in1", [D, 2 * DFF])]
    ffo_d = [din("ffn_out0", [DFF, D]), din("ffn_out1", [DFF, D])]
    cvi_d = din("conv_in", [D, 3 * D])
    cvo_d = din("conv_out", [D, D])
    kvm_d = din("kv_mod_w", [D, 2 * D])
    wkv_d = din("w_kv", [D, 2 * D])
    wq_d = din("w_q", [D, D])
    wo_d = din("w_o", [D, D])
    out_d = nc.dram_tensor("out", [OWN, D], F32, kind="ExternalOutput").ap()

    S = Sched()
    with ExitStack() as es:
        def sb(name, shape, dt):
            return es.enter_context(nc.sbuf_tensor(name, list(shape), dt))

        xs = sb("xs", [128, NH, KC, HT], F32)
        xin = [sb(f"xin{i}", [128, D], F32) for i in range(2)]
        hbuf = sb("hbuf", [128, NH, KC, HT], BF16)
        tmp = sb("tmp", [128, NH, 2, HT], F32)
        sq = sb("sq", [128, NH, KC, HT], BF16)
        rstd = sb("rstd", [128, NH, HT], F32)
        zb = sb("zb", [128, NH, 2, HT + 2], F32)
        csb = sb("csb", [128, NH, 2, HT], F32)
        acc = sb("acc", [128, NH, 2, HT], F32)
        ztail = sb("ztail", [128, KC, 2], F32)
        cpre = sb("cpre", [128, 2], F32)
        qu_f = sb("qu", [128, NH, 8 * HT], F32)
        qu_b = qu_f.bitcast(BF16)
        ab_f = sb("abuf", [128, NH, 11 * HT], F32)
        ab_b = ab_f.bitcast(BF16)
        silu_t = sb("silu", [128, NH, 2, HT], F32)
        Kb = sb("Kb", [128, KC, NKS, HT], BF16)
        Vb = sb("Vb", [128, 8, D], BF16)
        Pt = sb("Pt", [128, 2, 640], BF16)
        rc = sb("rc", [128, NH, HT], F32)
        bhi = sb("bhi", [128, 16, 2, 128], BF16)
        blo = sb("blo", [128, 16, 2, 128], BF16)
        mask0 = sb("mask0", [128, 128], BF16)
        ident = sb("ident_f", [128, 128], F32)
        identb = sb("ident_b", [128, 128], BF16)
        onesmean = sb("onesmean", [128, 128], BF16)
        ones_bf = sb("ones_bf", [128, 64], BF16)
        hv_bf = sb("hv_bf", [128, 64], BF16)
        vecs = sb("vecs_sb", [128, NV], F32)
        cbs = sb("cb_sb", [128, 16], F32)
        hv = sb("hv_sb", [128, 1], F32)
        modv = sb("modv", [128, 2, 48], F32)
        kvmod = sb("kvmod", [128, 16], F32)
        dv = sb("dv", [128, 2, 4, 8], F32)
        akv = sb("akv", [128, 8], F32)
        siluc = sb("siluc", [128, 8], BF16)
        xpre = sb("xpre_sb", [128, KC, 2], F32)
        sqpre = sb("sqpre", [128, KC, 2], BF16)
        rpre = sb("rpre", [128, 2], F32)
        tpre = sb("tpre", [128, KC, 2], F32)
        hpre = sb("hpre", [128, KC, 2], BF16)
        wring = sb("wring", [128, NSLOT, 8, 256], BF16)
        print("sbuf bytes remaining:", nc.sbuf_bytes_remaining)

        pb = [es.enter_context(nc.psum_tensor(f"pb{i}", [128, 512], F32)) for i in range(8)]
        bank_ctr = [0]

        def nb():
            b = bank_ctr[0] % 6
            bank_ctr[0] += 1
            return b

        def q_ap(hb, k, rows=slice(0, 128), cols=slice(0, HT)):
            return qu_b[rows, hb, k * HT + cols.start: k * HT + cols.stop]

        def u_ap(hb, k, rows=slice(0, 128), cols=slice(0, HT)):
            return qu_b[rows, hb, 8 * HT + k * HT + cols.start: 8 * HT + k * HT + cols.stop]

        def a_ap(hb, m):
            return ab_b[:, hb, m * HT:(m + 1) * HT]

        def y1_ap(hb, k):
            return ab_f[:, hb, k * HT:(k + 1) * HT]

        def y2_ap(hb, k):
            return qu_f[:, hb, k * HT:(k + 1) * HT]

        def y1_units(hb, k):
            return [("a", hb, 2 * k), ("a", hb, 2 * k + 1)]

        def y2_units(hb, k):
            if k < 4:
                return [("q", hb, 2 * k), ("q", hb, 2 * k + 1)]
            return [("u", hb, 2 * (k - 4)), ("u", hb, 2 * (k - 4) + 1)]

        def xout_ap(hb, s):
            return ab_f[:, hb, s * 1024:(s + 1) * 1024]

        def xout_units(hb, s):
            n = 4096 // (HT * 2)
            return [("a", hb, n * s + i) for i in range(n)]

        def vcol(c):
            return vecs[:, c:c + 1]

        def psu(b):
            return ("ps", b)

        def mm(out, lhsT, rhs, start, stop, reads, bank):
            S.add("pe", lambda e: e.matmul(out=out, lhsT=lhsT, rhs=rhs, start=start, stop=stop),
                  reads=reads, writes=[psu(bank)])

        def act(out, in_, func, reads, writes, bias=None, scale=None):
            kw = {}
            if bias is not None:
                kw["bias"] = bias
            if scale is not None:
                kw["scale"] = scale
            S.add("act", lambda e: e.activation(out=out, in_=in_, func=func, **kw), reads=reads, writes=writes)

        def dve(fn, reads, writes):
            S.add("dve", fn, reads=reads, writes=writes)

        def emit_program(ring):
            bank_ctr[0] = 0
            xpref = set()
            def spdma(out, in_, wunits, chan):
                S.add("sp", lambda e: e.dma_start(out=out, in_=in_), reads=[], writes=wunits, dma=True, chan=chan)

            spdma(vecs[:], vecs_d, ["vecs"], "c_vecs")
            spdma(ident[:], ident_d, ["ident"], "c_ident")
            spdma(hv[:], hv_d, ["hv"], "c_hv")
            spdma(cbs[:], cb_d, ["cbs"], "c_cb")
            spdma(xpre[:].rearrange("p k t -> p (k t)"), xpre_d, ["xpre"], "c_xpre")
            stage_units = [("a", hb, m) for hb in range(NH) for m in range(22)]
            stg = ab_f[:].rearrange("p h w -> p (h w)")[:, 0:4096]
            spdma(stg, bt_d, stage_units, "c_bt")
            ring.start()
            dve(lambda e: e.tensor_copy(out=identb[:], in_=ident[:]), ["ident"], ["identb"])
            dve(lambda e: e.memset(onesmean[:], 1.0 / D), [], ["onesmean"])
            dve(lambda e: e.memset(ones_bf[:], 1.0), [], ["ones_bf"])
            dve(lambda e: e.tensor_scalar(out=hv_bf[:], in0=ones_bf[:], scalar1=hv[:, 0:1], scalar2=None, op0=ALU.mult),
                ["ones_bf", "hv"], ["hv_bf"])
            dve(lambda e: e.memset(mask0[:], 0.0), [], ["mask0"])
            dve(lambda e: e.memset(mask0[0:64, 64:128], NEG), ["mask0"], ["mask0"])
            bhi_flat = bhi[:].rearrange("p h a j -> p (h a j)")
            blo_flat = blo[:].rearrange("p h a j -> p (h a j)")
            dve(lambda e: e.tensor_copy(out=bhi_flat, in_=stg), stage_units, ["bhi"])
            dve(lambda e: e.tensor_tensor(out=stg, in0=stg, in1=bhi_flat, op=ALU.subtract), stage_units + ["bhi"], stage_units)
            dve(lambda e: e.tensor_copy(out=blo_flat, in_=stg), stage_units, ["blo"])
            dve(lambda e: e.memset(bhi[64:128, :, 1, 0:64], NEG), ["bhi"], ["bhi"])
            dve(lambda e: e.memset(blo[64:128, :, 1, 0:64], 0.0), ["blo"], ["blo"])
            act(siluc[:], vecs[:, V_C:V_C + 8], AF.Silu, ["vecs"], ["siluc"])

            def matvec(wd, ncols, out_ap, bias_ap, out_unit, c0=0):
                nblk = ncols // 256
                bank = nb()
                for bi in range(nblk):
                    blk = ring.take(wd, 0, 8, c0 + bi * 256, 256)
                    for oc in range(2):
                        col = bi * 2 + oc
                        for kc in range(KC):
                            mm(pb[bank][:, col:col + 1], wring[:, blk.slot, kc, oc * 128:(oc + 1) * 128], siluc[:, kc:kc + 1],
                               kc == 0, kc == KC - 1, [("w", blk.slot), "siluc"], bank)
                    ring.done(blk)
                dve(lambda e: e.tensor_tensor(out=out_ap, in0=pb[bank][:, 0:nblk * 2], in1=bias_ap, op=ALU.add),
                    [psu(bank), "vecs"], [psu(bank), out_unit])

            def derive_layer(l):
                m = modv[:, l, :]
                ng = lambda j: vecs[:, V_NG + (l * 4 + j) * 8: V_NG + (l * 4 + j) * 8 + 8]
                u = ("mod", l)
                dve(lambda e: e.scalar_tensor_tensor(out=dv[:, l, 0, :], in0=modv[:, l, 8:16], scalar=1.0, in1=ng(0), op0=ALU.add, op1=ALU.mult),
                    [u, "vecs"], [("dv", l, 0)])
                dve(lambda e: e.tensor_tensor(out=dv[:, l, 1, :], in0=modv[:, l, 16:24], in1=ng(1), op=ALU.mult), [u, "vecs"], [("dv", l, 1)])
                dve(lambda e: e.scalar_tensor_tensor(out=dv[:, l, 2, :], in0=modv[:, l, 32:40], scalar=1.0, in1=ng(2), op0=ALU.add, op1=ALU.mult),
                    [u, "vecs"], [("dv", l, 2)])
                dve(lambda e: e.tensor_tensor(out=dv[:, l, 3, :], in0=modv[:, l, 40:48], in1=ng(3), op=ALU.mult), [u, "vecs"], [("dv", l, 3)])

            matvec(modw_d[0], 6 * D, modv[:, 0, :], vecs[:, 0:48], ("mod", 0))
            derive_layer(0)

            def sq_from(hb, k, src_ap, src_units):
                act(sq[:, hb, k, :], src_ap, AF.Square, src_units, [("sq", hb, k)])

            def stat_mm(hb, k):
                mm(pb[6 + hb][:, 0:HT], onesmean[:], sq[:, hb, k, :], k == 0, k == KC - 1, ["onesmean", ("sq", hb, k)], 6 + hb)

            def rstd_from_stat(hb):
                act(rstd[:, hb, :], pb[6 + hb][:, 0:HT], AF.Ln, [psu(6 + hb)], [psu(6 + hb), ("rstd", hb)], bias=EPS, scale=1.0)
                act(rstd[:, hb, :], rstd[:, hb, :], AF.Exp, [("rstd", hb)], [("rstd", hb)], scale=-0.5)

            def postnorm(hb, l, gi, y_ap, y_units, need_sq):
                rstd_from_stat(hb)
                for k in range(KC):
                    r = k % 2
                    dve(lambda e, k=k, r=r: e.tensor_tensor(out=tmp[:, hb, r, :], in0=y_ap(hb, k), in1=rstd[:, hb, :], op=ALU.mult),
                        y_units(hb, k) + [("rstd", hb)], [("tmp", hb, r)])
                    dve(lambda e, k=k, r=r: e.scalar_tensor_tensor(out=xs[:, hb, k, :], in0=tmp[:, hb, r, :], scalar=dv[:, l, gi, k:k + 1],
                                                                   in1=xs[:, hb, k, :], op0=ALU.mult, op1=ALU.add),
                        [("tmp", hb, r), ("dv", l, gi), ("xs", hb, k)], [("xs", hb, k)])
                    if need_sq:
                        sq_from(hb, k, xs[:, hb, k, :], [("xs", hb, k)])

            def norm_sub(hb, affines):
                for k in range(KC):
                    stat_mm(hb, k)
                rstd_from_stat(hb)
                for k in range(KC):
                    r = k % 2
                    dve(lambda e, k=k, r=r: e.tensor_tensor(out=tmp[:, hb, r, :], in0=xs[:, hb, k, :], in1=rstd[:, hb, :], op=ALU.mult),
                        [("xs", hb, k), ("rstd", hb)], [("tmp", hb, r)])
                    for (Af, Bf, of, uf, xr) in affines:
                        act(of(k), tmp[:, hb, r, :], AF.Identity, [("tmp", hb, r)] + xr, [uf(k)], bias=Bf(k), scale=Af(k))

            def h_aff(l, ai, bcol):
                return (lambda k: dv[:, l, ai, k:k + 1], lambda k: modv[:, l, bcol + k: bcol + k + 1],
                        lambda k, hb=None: None, None, [("dv", l, ai), ("mod", l)])

            class Sub:
                def __init__(self, fn, rel=None, name="?"):
                    self.fn = fn
                    self.rel = rel
                    self.st = {}
                    self.name = name

            def dense_groups(st, hb, is_a, specs, rhs_fn, rhs_units_fn, epi):
                if is_a:
                    st["blks"] = [[ring.take(wd, r0, nkc, c0, 256) for (wd, r0, nkc, c0) in parts] for parts in specs["parts"]]
                pending = []
                if specs.get("kouter"):
                    grp = [(ci, oc, nb()) for ci in range(len(specs["parts"])) for oc in range(2)]
                    for kc in range(KC):
                        for ci, oc, bank in grp:
                            (wd, r0, nkc, c0) = specs["parts"][ci][0]
                            blk = st["blks"][ci][0]
                            mm(pb[bank][:, 0:HT], wring[:, blk.slot, kc, oc * 128:(oc + 1) * 128], rhs_fn(hb, r0 + kc),
                               kc == 0, kc == KC - 1, [("w", blk.slot), rhs_units_fn(hb, r0 + kc)], bank)
                    for ci, oc, bank in grp:
                        r = epi(specs["k0"] + ci * 2 + oc, bank)
                        if r is not None:
                            pending.append(r)
                    for p in pending:
                        p()
                    return
                for ci, parts in enumerate(specs["parts"]):
                    blks = st["blks"][ci]
                    for oc in range(2):
                        kglob = specs["k0"] + ci * 2 + oc
                        bank = nb()
                        total = sum(p[2] for p in parts)
                        cnt = 0
                        for (wd, r0, nkc, c0), blk in zip(parts, blks):
                            for kc in range(nkc):
                                mm(pb[bank][:, 0:HT], wring[:, blk.slot, kc, oc * 128:(oc + 1) * 128], rhs_fn(hb, r0 + kc),
                                   cnt == 0, cnt == total - 1, [("w", blk.slot), rhs_units_fn(hb, r0 + kc)], bank)
                                cnt += 1
                        r = epi(kglob, bank)
                        if r is not None:
                            pending.append(r)
                for p in pending:
                    p()

            def release(st):
                for lst in st.get("blks", []):
                    for blk in lst:
                        ring.done(blk)
                st["blks"] = []

            def build_pass(p):
                hidx_of = lambda hb: NH * p + hb
                subs = []
                l1 = do_l1 and p >= 1

                def f_xl(hb, is_a, st):
                    hidx = hidx_of(hb)
                    def xload(hx, s):
                        row0 = hx * HT + s * 128
                        xb = s % 2
                        S.add("sp", lambda e, xb=xb, row0=row0: e.dma_start(out=xin[xb][:], in_=x_d[row0:row0 + 128, :]),
                              reads=[], writes=[("xin", xb)], dma=True, chan=("xin", xb))
                    for s in range(NST):
                        xb = s % 2
                        if (hidx, s) not in xpref:
                            xload(hidx, s)
                        for g in range(2):
                            bank = nb()
                            for kk in range(4):
                                k = g * 4 + kk
                                S.add("pe", lambda e, xb=xb, k=k, kk=kk, bank=bank: e.transpose(out=pb[bank][:, kk * 128:(kk + 1) * 128],
                                                                                             in_=xin[xb][:, k * 128:(k + 1) * 128], identity=ident[:]),
                                      reads=[("xin", xb), "ident"], writes=[psu(bank)])
                            dst = xs[:, hb, g * 4:(g + 1) * 4, s * 128:(s + 1) * 128]
                            src = pb[bank][:, :].rearrange("p (a t) -> p a t", a=4)
                            wu = [psu(bank)] + [("xs", hb, g * 4 + kk) for kk in range(4)]
                            if g == 0:
                                S.add("act", lambda e, dst=dst, src=src: e.copy(out=dst, in_=src), reads=[psu(bank)], writes=wu)
                            else:
                                dve(lambda e, dst=dst, src=src: e.tensor_copy(out=dst, in_=src), [psu(bank)], wu)
                    if hidx + 1 < nhalf:
                        for s in range(2):
                            xload(hidx + 1, s)
                            xpref.add((hidx + 1, s))
                    for k in range(KC):
                        sq_from(hb, k, xs[:, hb, k, :], [("xs", hb, k)])
                    if hidx == 0:
                        act(sqpre[:].rearrange("p k t -> p (k t)"), xpre[:].rearrange("p k t -> p (k t)"), AF.Square, ["xpre"], ["sqpre"])
                        bank = nb()
                        for k in range(KC):
                            mm(pb[bank][:, 0:2], onesmean[:], sqpre[:, k, :], k == 0, k == KC - 1, ["onesmean", "sqpre"], bank)
                        act(rpre[:], pb[bank][:, 0:2], AF.Ln, [psu(bank)], [psu(bank), "rpre"], bias=EPS, scale=1.0)
                        act(rpre[:], rpre[:], AF.Exp, ["rpre"], ["rpre"], scale=-0.5)
                        for k in range(KC):
                            dve(lambda e, k=k: e.tensor_tensor(out=tpre[:, k, :], in0=xpre[:, k, :], in1=rpre[:], op=ALU.mult),
                                ["xpre", "rpre"], [("tpre", k)])
                            act(hpre[:, k, :], tpre[:, k, :], AF.Identity, [("tpre", k), ("dv", 0, 0), ("mod", 0)], [("hpre", k)],
                                bias=modv[:, 0, k:k + 1], scale=dv[:, 0, 0, k:k + 1])
                subs.append(Sub(f_xl, name="xl"))

                def f_n1(hb, is_a, st):
                    norm_sub(hb, [(lambda k: dv[:, 0, 0, k:k + 1], lambda k: modv[:, 0, k:k + 1],
                                   lambda k: hbuf[:, hb, k, :], lambda k: ("h", hb, k), [("dv", 0, 0), ("mod", 0)])])
                subs.append(Sub(f_n1, name="n1"))

                def mk_mx(pr):
                    def f(hb, is_a, st):
                        hidx = hidx_of(hb)
                        if is_a:
                            st["b"] = ring.take(cvi_d, 0, 8, pr * 256, 256)
                            st["c"] = ring.take(cvi_d, 0, 8, D + pr * 256, 256)
                            st["x"] = ring.take(cvi_d, 0, 8, 2 * D + pr * 256, 256)
                        pre = {}
                        kout = pr == 0 and hidx != 0
                        if kout:
                            for oc in range(2):
                                for nm in ("c", "x", "b"):
                                    pre[(oc, nm)] = nb()
                            for kc in range(KC):
                                for oc in range(2):
                                    for nm in ("c", "x", "b"):
                                        blk = st[nm]
                                        bank = pre[(oc, nm)]
                                        mm(pb[bank][:, 0:HT], wring[:, blk.slot, kc, oc * 128:(oc + 1) * 128], hbuf[:, hb, kc, :],
                                           kc == 0, kc == KC - 1, [("w", blk.slot), ("h", hb, kc)], bank)
                        for oc in range(2):
                            j = 2 * pr + oc
                            r = j % 2
                            banks = {}
                            for nm in ("c", "x", "b"):
                                if kout:
                                    banks[nm] = pre[(oc, nm)]
                                    continue
                                bank = nb()
                                banks[nm] = bank
                                blk = st[nm]
                                for kc in range(KC):
                                    mm(pb[bank][:, 0:HT], wring[:, blk.slot, kc, oc * 128:(oc + 1) * 128], hbuf[:, hb, kc, :],
                                       kc == 0, kc == KC - 1, [("w", blk.slot), ("h", hb, kc)], bank)
                            if hidx == 0:
                                bp = nb()
                                for gi, nm in enumerate(("c", "x")):
                                    blk = st[nm]
                                    for kc in range(KC):
                                        mm(pb[bp][:, 2 * gi:2 * gi + 2], wring[:, blk.slot, kc, oc * 128:(oc + 1) * 128], hpre[:, kc, :],
                                           kc == 0, kc == KC - 1, [("w", blk.slot), ("hpre", kc)], bp)
                                act(cpre[:], pb[bp][:, 0:2], AF.Copy, [psu(bp)], [psu(bp), "cpre"])
                                dve(lambda e, j=j, bp=bp: e.tensor_tensor(out=ztail[:, j, :], in0=cpre[:], in1=pb[bp][:, 2:4], op=ALU.mult),
                                    ["cpre", psu(bp)], [psu(bp), ("ztail", j)])
                            bc, bx, bbk = banks["c"], banks["x"], banks["b"]
                            act(csb[:, hb, r, :], pb[bc][:, 0:HT], AF.Copy, [psu(bc)], [psu(bc), ("csb", hb, r)])
                            dve(lambda e, j=j, r=r: e.tensor_copy(out=zb[:, hb, r, 0:2], in_=ztail[:, j, :]), [("ztail", j)], [("zb", hb, r)])
                            dve(lambda e, r=r, bx=bx: e.tensor_tensor(out=zb[:, hb, r, 2:HT + 2], in0=csb[:, hb, r, :], in1=pb[bx][:, 0:HT], op=ALU.mult),
                                [("csb", hb, r), psu(bx), ("zb", hb, r)], [psu(bx), ("zb", hb, r)])
                            if hidx == NHALO - 1:
                                dve(lambda e, j=j, r=r: e.tensor_scalar(out=ztail[:, j, :], in0=zb[:, hb, r, HT:HT + 2], scalar1=hv[:, 0:1], scalar2=None, op0=ALU.mult),
                                    [("zb", hb, r), "hv"], [("ztail", j)])
                            else:
                                dve(lambda e, j=j, r=r: e.tensor_copy(out=ztail[:, j, :], in_=zb[:, hb, r, HT:HT + 2]), [("zb", hb, r)], [("ztail", j)])
                            ck = lambda t, j=j: vecs[:, V_CK + t * 8 + j: V_CK + t * 8 + j + 1]
                            act(acc[:, hb, r, :], zb[:, hb, r, 2:HT + 2], AF.Identity, [("zb", hb, r), "vecs"], [("acc", hb, r)], scale=ck(2))
                            dve(lambda e, r=r, ck=ck: e.scalar_tensor_tensor(out=acc[:, hb, r, :], in0=zb[:, hb, r, 1:HT + 1], scalar=ck(1), in1=acc[:, hb, r, :],
                                                                             op0=ALU.mult, op1=ALU.add), [("zb", hb, r), ("acc", hb, r), "vecs"], [("acc", hb, r)])
                            dve(lambda e, r=r, ck=ck: e.scalar_tensor_tensor(out=acc[:, hb, r, :], in0=zb[:, hb, r, 0:HT], scalar=ck(0), in1=acc[:, hb, r, :],
                                                                             op0=ALU.mult, op1=ALU.add), [("zb", hb, r), ("acc", hb, r), "vecs"], [("acc", hb, r)])
                            dve(lambda e, j=j, r=r, bbk=bbk: e.tensor_tensor(out=u_ap(hb, j), in0=pb[bbk][:, 0:HT], in1=acc[:, hb, r, :], op=ALU.mult),
                                [psu(bbk), ("acc", hb, r)], [psu(bbk), ("u", hb, j)])

                    def rel(st):
                        for nm in ("b", "c", "x"):
                            ring.done(st[nm])
                    return Sub(f, rel, name="mx")
                for pr in range(4):
                    subs.append(mk_mx(pr))

                def mk_proj_post(wd, nkparts, cbks, rhs_fn, rhs_units_fn, y_ap, y_units, l, gi, last, need_sq, nm):
                    def f(hb, is_a, st):
                        parts = []
                        for cbk in cbks:
                            parts.append([(wd, r0, nkc, cbk * 256) for (r0, nkc) in nkparts])

                        def epi(k, bank):
                            act(y_ap(hb, k), pb[bank][:, 0:HT], AF.Copy, [psu(bank)], [psu(bank)] + y_units(hb, k))
                            act(sq[:, hb, k, :], pb[bank][:, 0:HT], AF.Square, [psu(bank)], [psu(bank), ("sq", hb, k)])
                            return lambda: stat_mm(hb, k)
                        dense_groups(st, hb, is_a, {"parts": parts, "k0": 2 * cbks[0]}, rhs_fn, rhs_units_fn, epi)
                        if last:
                            postnorm(hb, l, gi, y_ap, y_units, need_sq)
                    return Sub(f, release, name=nm)

                u_rhs = lambda hb, kc: u_ap(hb, kc)
                u_units = lambda hb, kc: ("u", hb, kc)
                a_rhs = lambda hb, m: a_ap(hb, m)
                a_units = lambda hb, m: ("a", hb, m)
                h_rhs = lambda hb, kc: hbuf[:, hb, kc, :]
                h_units = lambda hb, kc: ("h", hb, kc)

                def mk_ffn1(l, pairs):
                    def f(hb, is_a, st):
                        if is_a:
                            st["blks"] = [[ring.take(ffi_d[l], 0, 8, pr * 256, 256), ring.take(ffi_d[l], 0, 8, DFF + pr * 256, 256)] for pr in pairs]
                        for pi, pr in enumerate(pairs):
                            gb, ub = st["blks"][pi]
                            pre = {}
                            if pr == 0:
                                for oc in range(2):
                                    pre[(oc, 0)] = nb()
                                    pre[(oc, 1)] = nb()
                                for kc in range(KC):
                                    for oc in range(2):
                                        for wi, wb_ in enumerate((gb, ub)):
                                            bank = pre[(oc, wi)]
                                            mm(pb[bank][:, 0:HT], wring[:, wb_.slot, kc, oc * 128:(oc + 1) * 128], hbuf[:, hb, kc, :],
                                               kc == 0, kc == KC - 1, [("w", wb_.slot), ("h", hb, kc)], bank)
                            for oc in range(2):
                                m = 2 * pr + oc
                                r = m % 2
                                if pr == 0:
                                    bg, bu = pre[(oc, 0)], pre[(oc, 1)]
                                else:
                                    bg = nb()
                                    for kc in range(KC):
                                        mm(pb[bg][:, 0:HT], wring[:, gb.slot, kc, oc * 128:(oc + 1) * 128], hbuf[:, hb, kc, :],
                                           kc == 0, kc == KC - 1, [("w", gb.slot), ("h", hb, kc)], bg)
                                    bu = nb()
                                    for kc in range(KC):
                                        mm(pb[bu][:, 0:HT], wring[:, ub.slot, kc, oc * 128:(oc + 1) * 128], hbuf[:, hb, kc, :],
                                           kc == 0, kc == KC - 1, [("w", ub.slot), ("h", hb, kc)], bu)
                                act(silu_t[:, hb, r, :], pb[bg][:, 0:HT], AF.Silu, [psu(bg)], [psu(bg), ("silu", hb, r)])
                                dve(lambda e, m=m, r=r, bu=bu: e.tensor_tensor(out=a_ap(hb, m), in0=silu_t[:, hb, r, :], in1=pb[bu][:, 0:HT], op=ALU.mult),
                                    [("silu", hb, r), psu(bu)], [psu(bu), ("a", hb, m)])
                    return Sub(f, release, name="f1_%d" % l)

                def ffn_subs(l, final_sq):
                    out = []
                    out.append(Sub(lambda hb, is_a, st: norm_sub(hb, [(lambda k: dv[:, l, 2, k:k + 1], lambda k: modv[:, l, 24 + k:25 + k],
                                                                       lambda k: hbuf[:, hb, k, :], lambda k: ("h", hb, k), [("dv", l, 2), ("mod", l)])]), name="n2_%d" % l))
                    for pairs in ([0, 1], [2, 3], [4, 5], [6, 7], [8, 9], [10]):
                        out.append(mk_ffn1(l, pairs))
                    kparts = [(0, 8), (8, 8), (16, 6)]
                    for cb in range(4):
                        out.append(mk_proj_post(ffo_d[l], kparts, [cb], a_rhs, a_units, y2_ap, y2_units, l, 3, cb == 3, final_sq, "f2_%d" % l))
                    return out

                subs.append(mk_proj_post(cvo_d, [(0, 8)], [0, 1], u_rhs, u_units, y1_ap, y1_units, 0, 1, False, True, "wo0"))
                subs.append(mk_proj_post(cvo_d, [(0, 8)], [2, 3], u_rhs, u_units, y1_ap, y1_units, 0, 1, True, True, "wo0"))
                subs.extend(ffn_subs(0, True))

                if p == 0:
                    def f_kvmod(hb, is_a, st):
                        if not is_a:
                            return
                        matvec(kvm_d, 2 * D, kvmod[:], vecs[:, V_KVB:V_KVB + 16], "kvmod")
                        dve(lambda e: e.scalar_tensor_tensor(out=akv[:], in0=kvmod[:, 8:16], scalar=1.0, in1=vecs[:, V_KVG:V_KVG + 8], op0=ALU.add, op1=ALU.mult),
                            ["kvmod", "vecs"], ["akv"])
                    subs.append(Sub(f_kvmod, name="kvmod"))
                if p == 1 and do_l1:
                    def mk_mod1(part):
                        def f_mod1(hb, is_a, st):
                            if not is_a:
                                return
                            matvec(modw_d[1], 2048, modv[:, 1, 16 * part:16 * part + 16], vecs[:, 48 + 16 * part:48 + 16 * part + 16],
                                   ("mod", 1), c0=2048 * part)
                            if part == 2:
                                derive_layer(1)
                        return Sub(f_mod1, name="mod1")
                    for part in range(3):
                        subs.append(mk_mod1(part))

                def f_nkv(hb, is_a, st):
                    affs = [(lambda k: akv[:, k:k + 1], lambda k: kvmod[:, k:k + 1], lambda k: u_ap(hb, k), lambda k: ("u", hb, k), ["akv", "kvmod"])]
                    if l1:
                        affs.append((lambda k: dv[:, 1, 0, k:k + 1], lambda k: modv[:, 1, k:k + 1], lambda k: hbuf[:, hb, k, :], lambda k: ("h", hb, k),
                                     [("dv", 1, 0), ("mod", 1)]))
                    norm_sub(hb, affs)
                subs.append(Sub(f_nkv, name="nkv"))

                def mk_kk(half_idx):
                    def f(hb, is_a, st):
                        hidx = hidx_of(hb)
                        slot4 = hidx % NKS
                        parts = [[(wkv_d, 0, 8, cbk * 256)] for cbk in (2 * half_idx, 2 * half_idx + 1)]

                        def epi(k, bank):
                            dve(lambda e: e.tensor_copy(out=Kb[:, k, slot4, :], in_=pb[bank][:, 0:HT]), [psu(bank)], [psu(bank), ("K", slot4, k)])
                            return None
                        dense_groups(st, hb, is_a, {"parts": parts, "k0": 4 * half_idx, "kouter": half_idx == 0}, u_rhs, u_units, epi)
                    return Sub(f, release, name="kk")
                subs.append(mk_kk(0))
                subs.append(mk_kk(1))

                def mk_vv(half_idx):
                    def f(hb, is_a, st):
                        hidx = hidx_of(hb)
                        slot4 = hidx % NKS
                        if is_a:
                            st["blks"] = [[ring.take(wkv_d, 0, 8, D + cbk * 256, 256)] for cbk in (2 * half_idx, 2 * half_idx + 1)]
                        for ci, cbk in enumerate((2 * half_idx, 2 * half_idx + 1)):
                            blk = st["blks"][ci][0]
                            for s in range(NST):
                                bank = nb()
                                for kc in range(KC):
                                    mm(pb[bank][:, 0:256], u_ap(hb, kc, cols=slice(s * 128, (s + 1) * 128)), wring[:, blk.slot, kc, :],
                                       kc == 0, kc == KC - 1, [("w", blk.slot), ("u", hb, kc)], bank)
                                vt = slot4 * NST + s
                                if hidx < NHALO:
                                    S.add("act", lambda e, vt=vt, cbk=cbk, bank=bank: e.activation(out=Vb[:, vt, cbk * 256:(cbk + 1) * 256], in_=pb[bank][:, 0:256],
                                                                                                   func=AF.Identity, scale=hv[:, 0:1]),
                                          reads=[psu(bank), "hv"], writes=[psu(bank), ("V", vt, cbk)])
                                else:
                                    act(Vb[:, vt, cbk * 256:(cbk + 1) * 256], pb[bank][:, 0:256], AF.Copy, [psu(bank)], [psu(bank), ("V", vt, cbk)])
                    return Sub(f, release, name="vv")
                subs.append(mk_vv(0))
                subs.append(mk_vv(1))

                if l1:
                    def mk_qq(half_idx):
                        def f(hb, is_a, st):
                            parts = [[(wq_d, 0, 8, cbk * 256)] for cbk in (2 * half_idx, 2 * half_idx + 1)]

                            def epi(k, bank):
                                S.add("act", lambda e: e.mul(out=q_ap(hb, k), in_=pb[bank][:, 0:HT], mul=0.125), reads=[psu(bank)], writes=[psu(bank), ("q", hb, k)])
                                return None
                            dense_groups(st, hb, is_a, {"parts": parts, "k0": 4 * half_idx}, h_rhs, h_units, epi)
                        return Sub(f, release, name="qq")
                    subs.append(mk_qq(0))
                    subs.append(mk_qq(1))

                    def mk_at(hp):
                        def f(hb, is_a, st):
                            hidx = hidx_of(hb)

                            def key_loc(qi, kt):
                                ek = hidx * HT + qi * 128 - 512 + 128 * kt
                                hk = ek // HT
                                off = ek % HT
                                return hk, off, ek

                            for qh in range(NST // 2):
                                items = [(2 * qh + qq, hh) for qq in range(2) for hh in range(2)]
                                osb = 4 + ((hp * (NST // 2) + qh) % 2)

                                def scores(n):
                                    qi, hh = items[n]
                                    h = 2 * hp + hh
                                    rows = slice(64 * hh, 64 * hh + 64)
                                    b0 = 2 * (n % 2)
                                    b1 = b0 + 1
                                    for kt in (1, 2, 0, 3, 4):
                                        hk, off, ek = key_loc(qi, kt)
                                        s4 = hk % NKS
                                        bank, c0 = (b0, kt * 128) if kt < 3 else (b1, (kt - 3) * 128)
                                        outp = pb[bank][:, c0:c0 + 128]
                                        st_flag = kt != 4
                                        S.add("pe", lambda e, outp=outp, s4=s4, off=off, st_flag=st_flag, kt=kt, rows=rows, qi=qi: e.matmul(
                                            out=outp, lhsT=Kb[rows, hp, s4, off:off + 128],
                                            rhs=q_ap(hb, hp, rows=rows, cols=slice(qi * 128, qi * 128 + 128)),
                                            start=st_flag, stop=(kt in (1, 2)), skip_group_check=True),
                                            reads=[("K", s4, hp), ("q", hb, hp)], writes=[psu(bank)])
                                    S.add("pe", lambda e, b0=b0: e.matmul(out=pb[b0][:, 0:128], lhsT=identb[:], rhs=mask0[:], start=False, stop=True, skip_group_check=True),
                                          reads=["identb", "mask0"], writes=[psu(b0)])
                                    S.add("pe", lambda e, b1=b1, h=h: e.matmul(out=pb[b1][:, 0:256], lhsT=identb[:], rhs=bhi[:, h, :, :].rearrange("p a j -> p (a j)"),
                                                                             start=False, stop=False, skip_group_check=True), reads=["identb", "bhi"], writes=[psu(b1)])
                                    S.add("pe", lambda e, b1=b1, h=h: e.matmul(out=pb[b1][:, 0:256], lhsT=identb[:], rhs=blo[:, h, :, :].rearrange("p a j -> p (a j)"),
                                                                             start=False, stop=True, skip_group_check=True), reads=["identb", "blo"], writes=[psu(b1)])
                                    pi = n % 2
                                    act(Pt[:, pi, 0:384], pb[b0][:, 0:384], AF.Exp, [psu(b0), "cbs"], [psu(b0), ("P", pi)], bias=cbs[:, h:h + 1], scale=1.0)
                                    act(Pt[:, pi, 384:640], pb[b1][:, 0:256], AF.Exp, [psu(b1)], [psu(b1), ("P", pi)])

                                def pv(n):
                                    qi, hh = items[n]
                                    qq = qi % 2
                                    h = 2 * hp + hh
                                    rows = slice(64 * hh, 64 * hh + 64)
                                    pi = n % 2
                                    for kt in range(5):
                                        hk, off, ek = key_loc(qi, kt)
                                        vt = (hk % NKS) * NST + off // 128
                                        mm(pb[osb][rows, qq * 128:(qq + 1) * 128], Vb[:, vt, h * 64:(h + 1) * 64], Pt[:, pi, kt * 128:(kt + 1) * 128],
                                           kt == 0, kt == 4, [("V", vt, h // 4), ("P", pi)], osb)
                                    for kt in range(5):
                                        hk, off, ek = key_loc(qi, kt)
                                        onesl = hv_bf if ek < HALO else ones_bf
                                        mm(pb[osb][rows, 256 + qq * 128:256 + (qq + 1) * 128], onesl[:], Pt[:, pi, kt * 128:(kt + 1) * 128],
                                           kt == 0, kt == 4, ["hv_bf", "ones_bf", ("P", pi)], osb)

                                scores(0)
                                for n in range(1, len(items)):
                                    scores(n)
                                    pv(n - 1)
                                pv(len(items) - 1)
                                cs = slice(qh * 256, qh * 256 + 256)
                                dve(lambda e, osb=osb, cs=cs: e.reciprocal(out=rc[:, hb, cs], in_=pb[osb][:, 256:512]), [psu(osb)], [psu(osb), ("rc", hb, qh)])
                                dve(lambda e, osb=osb, cs=cs: e.tensor_tensor(out=u_ap(hb, hp, cols=cs), in0=pb[osb][:, 0:256], in1=rc[:, hb, cs], op=ALU.mult),
                                    [psu(osb), ("rc", hb, qh)], [psu(osb), ("u", hb, hp)])
                        return Sub(f, name="at")
                    for hp in range(8):
                        subs.append(mk_at(hp))

                    subs.append(mk_proj_post(wo_d, [(0, 8)], [0, 1], u_rhs, u_units, y1_ap, y1_units, 1, 1, False, True, "wo1"))
                    subs.append(mk_proj_post(wo_d, [(0, 8)], [2, 3], u_rhs, u_units, y1_ap, y1_units, 1, 1, True, True, "wo1"))
                    subs.extend(ffn_subs(1, False))

                if p >= 1:
                    def f_out(hb, is_a, st):
                        hidx = hidx_of(hb)
                        for s in range(NST):
                            for g in range(2):
                                bank = nb()
                                for kk in range(4):
                                    k = g * 4 + kk
                                    S.add("pe", lambda e, k=k, kk=kk, s=s, bank=bank: e.transpose(out=pb[bank][:, kk * 128:(kk + 1) * 128],
                                                                                                   in_=xs[:, hb, k, s * 128:(s + 1) * 128], identity=ident[:]),
                                          reads=[("xs", hb, k), "ident"], writes=[psu(bank)])
                                if g == 0:
                                    act(xout_ap(hb, s)[:, 0:512], pb[bank][:, :], AF.Copy, [psu(bank)], [psu(bank)] + xout_units(hb, s)[0:len(xout_units(hb, s)) // 2])
                                else:
                                    dve(lambda e, s=s, bank=bank: e.tensor_copy(out=xout_ap(hb, s)[:, 512:1024], in_=pb[bank][:, :]),
                                        [psu(bank)], [psu(bank)] + xout_units(hb, s)[len(xout_units(hb, s)) // 2:])
                            row0 = (hidx - NHALO) * HT + s * 128
                            S.add("sp", lambda e, s=s, row0=row0: e.dma_start(out=out_d[row0:row0 + 128, :], in_=xout_ap(hb, s)),
                                  reads=xout_units(hb, s), writes=[("out", row0)], dma=True, chan=("xout", hb, s))
                    subs.append(Sub(f_out, name="out"))
                return subs

            items = []
            for p in range(nhalf // NH):
                items.extend(build_pass(p))
            if skew and NH == 2:
                for idx in range(len(items) + 1):
                    if idx < len(items):
                        it = items[idx]
                        S.label = it.name + ":A"
                        it.fn(0, True, it.st)
                    if idx >= 1:
                        it = items[idx - 1]
                        S.label = it.name + ":B"
                        it.fn(1, False, it.st)
                        if it.rel is not None:
                            it.rel(it.st)
            else:
                for it in items:
                    for hb in range(NH):
                        S.label = it.name + (":A" if hb == 0 else ":B")
                        it.fn(hb, hb == 0, it.st)
                    if it.rel is not None:
                        it.rel(it.st)
            outs = [("out", r) for r in range(0, (nhalf - NHALO) * HT, 128)]
            S.add("sp", lambda e: e.nop(), reads=outs, writes=[])

        S.dry = True
        rec = WRing(S, wring, None)
        emit_program(rec)
        plan = rec.plan
        S.dry = False
        ring = WRing(S, wring, plan)
        emit_program(ring)
        assert ring.i_use == len(plan), (ring.i_use, len(plan))

        cnt = S.analyze()
        print("ops:", len(S.ops), "sems:", len(S.sem_keys), "max sem val:", max(cnt.values()))
        sems = {}
        for i, key in enumerate(S.sem_keys):
            sems[key] = es.enter_context(nc.semaphore(f"s{i}"))
        block = es.enter_context(nc.Block())
        S.emit(block, sems)
    nc._sched_ops = S.ops
    return nc


def _fm(v):
    v = np.asarray(v, dtype=np.float32).reshape(-1, KC, 128)
    return np.ascontiguousarray(v.transpose(2, 0, 1).reshape(128, -1))


def prepare_inputs(x, c, mod_w, mod_b, norm_g, ffn_w_in, ffn_w_out, conv_w_in, conv_k, conv_w_out, kv_mod_w, kv_mod_b,
                   kv_norm_g, w_kv, attn_w_q, attn_w_o, rel_bias):
    f = lambda a: np.ascontiguousarray(np.asarray(a, dtype=np.float32))
    x = f(x)
    B, Sq, _ = x.shape
    rb = f(rel_bias)[0]
    i = np.arange(128)[:, None]
    j = np.arange(128)[None, :]
    idx0 = np.clip(j - i, -128, 128) + 128
    idx1 = np.clip(128 + j - i, -128, 128) + 128
    bt = np.stack([rb[:, idx1], rb[:, idx0]], axis=1)
    bt = np.ascontiguousarray(bt.transpose(2, 0, 1, 3).reshape(128, 4096))
    cbv = np.ascontiguousarray(np.broadcast_to(rb[:, 256][None, :], (128, 16))).astype(np.float32)
    ident = np.eye(128, dtype=np.float32)
    shared = dict(
        cb=cbv, btiles=bt, ident=ident,
        mod_w0=f(mod_w[0]), mod_w1=f(mod_w[1]), ffn_in0=f(ffn_w_in[0]), ffn_in1=f(ffn_w_in[1]),
        ffn_out0=f(ffn_w_out[0]), ffn_out1=f(ffn_w_out[1]), conv_in=f(conv_w_in[0]), conv_out=f(conv_w_out[0]),
        kv_mod_w=f(kv_mod_w), w_kv=f(w_kv), w_q=f(attn_w_q[0]), w_o=f(attn_w_o[0]),
    )
    vec_common = np.concatenate([
        _fm(f(mod_b)[0].reshape(6, D)), _fm(f(mod_b)[1].reshape(6, D)),
        _fm(f(norm_g).reshape(8, D)), _fm(f(conv_k)[0]), _fm(f(kv_mod_b).reshape(2, D)), _fm(f(kv_norm_g).reshape(1, D)),
    ], axis=1)
    in_maps = []
    per_b = N_CORES // B
    for core in range(N_CORES):
        b = core // per_b
        qd = core % per_b
        t0 = qd * OWN
        xc = np.zeros((HALO + OWN, D), np.float32)
        xp = np.zeros((2, D), np.float32)
        if qd > 0:
            xc[:] = x[b, t0 - HALO:t0 + OWN]
            xp[:] = x[b, t0 - HALO - 2:t0 - HALO]
        else:
            xc[HALO:] = x[b, 0:OWN]
        xpre = np.ascontiguousarray(xp.reshape(2, KC, 128).transpose(2, 1, 0).reshape(128, 16))
        vecs = np.ascontiguousarray(np.concatenate([vec_common, _fm(f(c)[b].reshape(1, D))], axis=1))
        assert vecs.shape == (128, NV)
        hvv = np.full((128, 1), 1.0 if qd > 0 else 0.0, np.float32)
        m = dict(shared)
        m.update(x=xc, xpre=xpre, vecs=vecs, hv=hvv)
        in_maps.append(m)
    return in_maps


_NC_CACHE = {}


def kernel(x, c, mod_w, mod_b, norm_g, ffn_w_in, ffn_w_out, conv_w_in, conv_k, conv_w_out, kv_mod_w, kv_mod_b,
           kv_norm_g, w_kv, attn_w_q, attn_w_o, rel_bias):
    in_maps = prepare_inputs(x, c, mod_w, mod_b, norm_g, ffn_w_in, ffn_w_out, conv_w_in, conv_k, conv_w_out, kv_mod_w,
                             kv_mod_b, kv_norm_g, w_kv, attn_w_q, attn_w_o, rel_bias)
    if "nc" not in _NC_CACHE:
        _NC_CACHE["nc"] = build_nc(skew=False)
    nc = _NC_CACHE["nc"]
    res = run_bass_kernel_spmd(nc, in_maps, core_ids=list(range(N_CORES)))
    B, Sq, _ = np.asarray(x).shape
    out = np.empty((B, Sq, D), np.float32)
    per_b = N_CORES // B
    for core in range(N_CORES):
        b = core // per_b
        qd = core % per_b
        out[b, qd * OWN:(qd + 1) * OWN] = np.asarray(res.results[core]["out"], dtype=np.float32)
    return out
```

```python
import numpy as np
from contextlib import ExitStack
import concourse.bass as bass
import concourse.mybir as mybir
from concourse.bass_utils import run_bass_kernel_spmd

F32 = mybir.dt.float32
BF16 = mybir.dt.bfloat16
AF = mybir.ActivationFunctionType
ALU = mybir.AluOpType

D = 1024
KC = 8
HT = 512
NH = 512 // HT
NST = HT // 128
NKS = 1024 // HT
NHALF = 2560 // HT
NHALO = 512 // HT
DFF = 2816
HC = 22
NSLOT = 12
EPS = 1e-6
NEG = -30000.0
N_CORES = 8
OWN = 2048
HALO = 512

V_MODB = (0, 48)
V_NG = 96
V_CK = 160
V_KVB = 184
V_KVG = 200
V_C = 208
NV = 216

ENGS = ("pe", "act", "dve", "pool", "sp")


class Op:
    __slots__ = ("eng", "fn", "reads", "writes", "dma", "deps", "signal", "sigval", "idx", "chan", "label")

    def __init__(self, eng, fn, reads, writes, dma, chan):
        self.eng = eng
        self.fn = fn
        self.reads = reads
        self.writes = writes
        self.dma = dma
        self.chan = chan
        self.deps = []
        self.signal = False
        self.sigval = 0
        self.idx = -1


class Sched:
    def __init__(self):
        self.ops = []
        self.dry = False
        self.label = "pro"

    def add(self, eng, fn, reads=(), writes=(), dma=False, chan=None):
        if self.dry:
            return None
        op = Op(eng, fn, tuple(reads), tuple(writes), dma, chan)
        op.idx = len(self.ops)
        op.label = self.label
        self.ops.append(op)
        return op

    def analyze(self):
        last_w = {}
        readers = {}
        for op in self.ops:
            deps = {}
            for u in op.reads:
                w = last_w.get(u)
                if w is not None:
                    deps[w.idx] = (w, "raw")
            for u in op.writes:
                w = last_w.get(u)
                if w is not None and w.idx not in deps:
                    deps[w.idx] = (w, "waw")
                for r in readers.get(u, ()):
                    if r.idx not in deps:
                        deps[r.idx] = (r, "war")
            for u in op.reads:
                lst = readers.setdefault(u, [])
                if not op.dma:
                    lst[:] = [r for r in lst if r.dma or r.eng != op.eng]
                lst.append(op)
            for u in op.writes:
                last_w[u] = op
                readers[u] = []
            best = {}
            for d, kind in deps.values():
                if d is op:
                    continue
                if d.dma:
                    op.deps.append(d)
                    d.signal = True
                    continue
                if d.eng == op.eng and not op.dma:
                    if op.eng == "pe":
                        continue
                    if kind == "war":
                        continue
                b = best.get(d.eng)
                if b is None or d.idx > b.idx:
                    best[d.eng] = d
            for d in best.values():
                op.deps.append(d)
                d.signal = True
        cnt = {}
        for op in self.ops:
            if op.dma:
                op.signal = True
            if not op.signal:
                continue
            key = ("chan", op.chan) if op.dma else ("eng", op.eng)
            inc = 16 if op.dma else 1
            cnt[key] = cnt.get(key, 0) + inc
            op.sigval = cnt[key]
        self.sem_keys = list(cnt.keys())
        return cnt

    def emit(self, block, sems):
        per_eng = {e: [] for e in ENGS}
        for op in self.ops:
            per_eng[op.eng].append(op)

        def run(eng_name):
            def body(eng):
                waited = {}
                for op in per_eng[eng_name]:
                    need = {}
                    for d in op.deps:
                        key = ("chan", d.chan) if d.dma else ("eng", d.eng)
                        if d.sigval > need.get(key, 0):
                            need[key] = d.sigval
                    for key, v in need.items():
                        if waited.get(key, 0) >= v:
                            continue
                        eng.wait_ge(sems[key], v)
                        waited[key] = v
                    ins = op.fn(eng)
                    if op.signal:
                        key = ("chan", op.chan) if op.dma else ("eng", op.eng)
                        ins.then_inc(sems[key], 16 if op.dma else 1)
            return body

        block.tensor(run("pe"))
        block.scalar(run("act"))
        block.vector(run("dve"))
        block.gpsimd(run("pool"))
        block.sync(run("sp"))


class Blk:
    __slots__ = ("i", "slot")

    def __init__(self, i, slot):
        self.i = i
        self.slot = slot


class WRing:
    def __init__(self, S, wring, plan=None):
        self.S = S
        self.wring = wring
        self.record = plan is None
        self.plan = [] if plan is None else plan
        self.i_use = 0
        self.i_issue = 0
        self.done_flags = {}

    def _issue_ready(self):
        while self.i_issue < len(self.plan):
            i = self.i_issue
            if i >= NSLOT and not self.done_flags.get(i - NSLOT, False):
                break
            src, nkc, ncols = self.plan[i]
            slot = i % NSLOT
            wr = self.wring
            self.S.add("pool", lambda e, src=src, nkc=nkc, ncols=ncols, slot=slot: e.dma_start(
                out=wr[:, slot, 0:nkc, 0:ncols], in_=src), reads=[], writes=[("w", slot)], dma=True, chan=("w", slot))
            self.i_issue += 1

    def start(self):
        if not self.record:
            self._issue_ready()

    def take(self, wd, r0, nkc, c0, ncols):
        i = self.i_use
        self.i_use += 1
        if self.record:
            src = wd[r0 * 128:(r0 + nkc) * 128, c0:c0 + ncols].rearrange("(kc p) n -> p kc n", p=128)
            self.plan.append((src, nkc, ncols))
        else:
            assert i < self.i_issue, ("weight ring overflow: block taken before its DMA could be issued", i, self.i_issue)
        return Blk(i, i % NSLOT)

    def done(self, blk):
        if self.record:
            return
        self.done_flags[blk.i] = True
        self._issue_ready()


def build_nc(do_l1=True, skew=True, nhalf=NHALF):
    nc = bass.Bass("TRN2", target_bir_lowering=False)

    def din(name, shape):
        return nc.dram_tensor(name, list(shape), F32, kind="ExternalInput").ap()

    x_d = din("x", [HALO + OWN, D])
    xpre_d = din("xpre", [128, 16])
    vecs_d = din("vecs", [128, NV])
    cb_d = din("cb", [128, 16])
    hv_d = din("hv", [128, 1])
    bt_d = din("btiles", [128, 4096])
    ident_d = din("ident", [128, 128])
    modw_d = [din("mod_w0", [D, 6 * D]), din("mod_w1", [D, 6 * D])]
    ffi_d = [din("ffn_in0", [D, 2 * DFF]), din("ffn_in1", [D, 2 * DFF])]
    ffo_d = [din("ffn_out0", [DFF, D]), din("ffn_out1", [DFF, D])]
    cvi_d = din("conv_in", [D, 3 * D])
    cvo_d = din("conv_out", [D, D])
    kvm_d = din("kv_mod_w", [D, 2 * D])
    wkv_d = din("w_kv", [D, 2 * D])
    wq_d = din("w_q", [D, D])
    wo_d = din("w_o", [D, D])
    out_d = nc.dram_tensor("out", [OWN, D], F32, kind="ExternalOutput").ap()

    S = Sched()
    with ExitStack() as es:
        def sb(name, shape, dt):
            return es.enter_context(nc.sbuf_tensor(name, list(shape), dt))

        xs = sb("xs", [128, NH, KC, HT], F32)
        xin = [sb(f"xin{i}", [128, D], F32) for i in range(2)]
        hbuf = sb("hbuf", [128, NH, KC, HT], BF16)
        tmp = sb("tmp", [128, NH, 2, HT], F32)
        sq = sb("sq", [128, NH, KC, HT], BF16)
        rstd = sb("rstd", [128, NH, HT], F32)
        zb = sb("zb", [128, NH, 2, HT + 2], F32)
        csb = sb("csb", [128, NH, 2, HT], F32)
        acc = sb("acc", [128, NH, 2, HT], F32)
        ztail = sb("ztail", [128, KC, 2], F32)
        cpre = sb("cpre", [128, 2], F32)
        qu_f = sb("qu", [128, NH, 8 * HT], F32)
        qu_b = qu_f.bitcast(BF16)
        ab_f = sb("abuf", [128, NH, 11 * HT], F32)
        ab_b = ab_f.bitcast(BF16)
        silu_t = sb("silu", [128, NH, 2, HT], F32)
        Kb = sb("Kb", [128, KC, NKS, HT], BF16)
        Vb = sb("Vb", [128, 8, D], BF16)
        Pt = sb("Pt", [128, 2, 640], BF16)
        rc = sb("rc", [128, NH, HT], F32)
        bhi = sb("bhi", [128, 16, 2, 128], BF16)
        blo = sb("blo", [128, 16, 2, 128], BF16)
        mask0 = sb("mask0", [128, 128], BF16)
        ident = sb("ident_f", [128, 128], F32)
        identb = sb("ident_b", [128, 128], BF16)
        onesmean = sb("onesmean", [128, 128], BF16)
        ones_bf = sb("ones_bf", [128, 64], BF16)
        hv_bf = sb("hv_bf", [128, 64], BF16)
        vecs = sb("vecs_sb", [128, NV], F32)
        cbs = sb("cb_sb", [128, 16], F32)
        hv = sb("hv_sb", [128, 1], F32)
        modv = sb("modv", [128, 2, 48], F32)
        kvmod = sb("kvmod", [128, 16], F32)
        dv = sb("dv", [128, 2, 4, 8], F32)
        akv = sb("akv", [128, 8], F32)
        siluc = sb("siluc", [128, 8], BF16)
        xpre = sb("xpre_sb", [128, KC, 2], F32)
        sqpre = sb("sqpre", [128, KC, 2], BF16)
        rpre = sb("rpre", [128, 2], F32)
        tpre = sb("tpre", [128, KC, 2], F32)
        hpre = sb("hpre", [128, KC, 2], BF16)
        wring = sb("wring", [128, NSLOT, 8, 256], BF16)
        print("sbuf bytes remaining:", nc.sbuf_bytes_remaining)

        pb = [es.enter_context(nc.psum_tensor(f"pb{i}", [128, 512], F32)) for i in range(8)]
        bank_ctr = [0]

        def nb():
            b = bank_ctr[0] % 6
            bank_ctr[0] += 1
            return b

        def q_ap(hb, k, rows=slice(0, 128), cols=slice(0, HT)):
            return qu_b[rows, hb, k * HT + cols.start: k * HT + cols.stop]

        def u_ap(hb, k, rows=slice(0, 128), cols=slice(0, HT)):
            return qu_b[rows, hb, 8 * HT + k * HT + cols.start: 8 * HT + k * HT + cols.stop]

        def a_ap(hb, m):
            return ab_b[:, hb, m * HT:(m + 1) * HT]

        def y1_ap(hb, k):
            return ab_f[:, hb, k * HT:(k + 1) * HT]

        def y2_ap(hb, k):
            return qu_f[:, hb, k * HT:(k + 1) * HT]

        def y1_units(hb, k):
            return [("a", hb, 2 * k), ("a", hb, 2 * k + 1)]

        def y2_units(hb, k):
            if k < 4:
                return [("q", hb, 2 * k), ("q", hb, 2 * k + 1)]
            return [("u", hb, 2 * (k - 4)), ("u", hb, 2 * (k - 4) + 1)]

        def xout_ap(hb, s):
            return ab_f[:, hb, s * 1024:(s + 1) * 1024]

        def xout_units(hb, s):
            n = 4096 // (HT * 2)
            return [("a", hb, n * s + i) for i in range(n)]

        def vcol(c):
            return vecs[:, c:c + 1]

        def psu(b):
            return ("ps", b)

        def mm(out, lhsT, rhs, start, stop, reads, bank):
            S.add("pe", lambda e: e.matmul(out=out, lhsT=lhsT, rhs=rhs, start=start, stop=stop),
                  reads=reads, writes=[psu(bank)])

        def act(out, in_, func, reads, writes, bias=None, scale=None):
            kw = {}
            if bias is not None:
                kw["bias"] = bias
            if scale is not None:
                kw["scale"] = scale
            S.add("act", lambda e: e.activation(out=out, in_=in_, func=func, **kw), reads=reads, writes=writes)

        def dve(fn, reads, writes):
            S.add("dve", fn, reads=reads, writes=writes)

        def emit_program(ring):
            bank_ctr[0] = 0
            xpref = set()
            def spdma(out, in_, wunits, chan):
                S.add("sp", lambda e: e.dma_start(out=out, in_=in_), reads=[], writes=wunits, dma=True, chan=chan)

            spdma(vecs[:], vecs_d, ["vecs"], "c_vecs")
            spdma(ident[:], ident_d, ["ident"], "c_ident")
            spdma(hv[:], hv_d, ["hv"], "c_hv")
            spdma(cbs[:], cb_d, ["cbs"], "c_cb")
            spdma(xpre[:].rearrange("p k t -> p (k t)"), xpre_d, ["xpre"], "c_xpre")
            stage_units = [("a", hb, m) for hb in range(NH) for m in range(22)]
            stg = ab_f[:].rearrange("p h w -> p (h w)")[:, 0:4096]
            spdma(stg, bt_d, stage_units, "c_bt")
            ring.start()
            dve(lambda e: e.tensor_copy(out=identb[:], in_=ident[:]), ["ident"], ["identb"])
            dve(lambda e: e.memset(onesmean[:], 1.0 / D), [], ["onesmean"])
            dve(lambda e: e.memset(ones_bf[:], 1.0), [], ["ones_bf"])
            dve(lambda e: e.tensor_scalar(out=hv_bf[:], in0=ones_bf[:], scalar1=hv[:, 0:1], scalar2=None, op0=ALU.mult),
                ["ones_bf", "hv"], ["hv_bf"])
            dve(lambda e: e.memset(mask0[:], 0.0), [], ["mask0"])
            dve(lambda e: e.memset(mask0[0:64, 64:128], NEG), ["mask0"], ["mask0"])
            bhi_flat = bhi[:].rearrange("p h a j -> p (h a j)")
            blo_flat = blo[:].rearrange("p h a j -> p (h a j)")
            dve(lambda e: e.tensor_copy(out=bhi_flat, in_=stg), stage_units, ["bhi"])
            dve(lambda e: e.tensor_tensor(out=stg, in0=stg, in1=bhi_flat, op=ALU.subtract), stage_units + ["bhi"], stage_units)
            dve(lambda e: e.tensor_copy(out=blo_flat, in_=stg), stage_units, ["blo"])
            dve(lambda e: e.memset(bhi[64:128, :, 1, 0:64], NEG), ["bhi"], ["bhi"])
            dve(lambda e: e.memset(blo[64:128, :, 1, 0:64], 0.0), ["blo"], ["blo"])
            act(siluc[:], vecs[:, V_C:V_C + 8], AF.Silu, ["vecs"], ["siluc"])

            def matvec(wd, ncols, out_ap, bias_ap, out_unit, c0=0):
                nblk = ncols // 256
                bank = nb()
                for bi in range(nblk):
                    blk = ring.take(wd, 0, 8, c0 + bi * 256, 256)
                    for oc in range(2):
                        col = bi * 2 + oc
                        for kc in range(KC):
                            mm(pb[bank][:, col:col + 1], wring[:, blk.slot, kc, oc * 128:(oc + 1) * 128], siluc[:, kc:kc + 1],
                               kc == 0, kc == KC - 1, [("w", blk.slot), "siluc"], bank)
                    ring.done(blk)
                dve(lambda e: e.tensor_tensor(out=out_ap, in0=pb[bank][:, 0:nblk * 2], in1=bias_ap, op=ALU.add),
                    [psu(bank), "vecs"], [psu(bank), out_unit])

            def derive_layer(l):
                m = modv[:, l, :]
                ng = lambda j: vecs[:, V_NG + (l * 4 + j) * 8: V_NG + (l * 4 + j) * 8 + 8]
                u = ("mod", l)
                dve(lambda e: e.scalar_tensor_tensor(out=dv[:, l, 0, :], in0=modv[:, l, 8:16], scalar=1.0, in1=ng(0), op0=ALU.add, op1=ALU.mult),
                    [u, "vecs"], [("dv", l, 0)])
                dve(lambda e: e.tensor_tensor(out=dv[:, l, 1, :], in0=modv[:, l, 16:24], in1=ng(1), op=ALU.mult), [u, "vecs"], [("dv", l, 1)])
                dve(lambda e: e.scalar_tensor_tensor(out=dv[:, l, 2, :], in0=modv[:, l, 32:40], scalar=1.0, in1=ng(2), op0=ALU.add, op1=ALU.mult),
                    [u, "vecs"], [("dv", l, 2)])
                dve(lambda e: e.tensor_tensor(out=dv[:, l, 3, :], in0=modv[:, l, 40:48], in1=ng(3), op=ALU.mult), [u, "vecs"], [("dv", l, 3)])

            matvec(modw_d[0], 6 * D, modv[:, 0, :], vecs[:, 0:48], ("mod", 0))
            derive_layer(0)

            def sq_from(hb, k, src_ap, src_units):
                act(sq[:, hb, k, :], src_ap, AF.Square, src_units, [("sq", hb, k)])

            def stat_mm(hb, k):
                mm(pb[6 + hb][:, 0:HT], onesmean[:], sq[:, hb, k, :], k == 0, k == KC - 1, ["onesmean", ("sq", hb, k)], 6 + hb)

            def rstd_from_stat(hb):
                act(rstd[:, hb, :], pb[6 + hb][:, 0:HT], AF.Ln, [psu(6 + hb)], [psu(6 + hb), ("rstd", hb)], bias=EPS, scale=1.0)
                act(rstd[:, hb, :], rstd[:, hb, :], AF.Exp, [("rstd", hb)], [("rstd", hb)], scale=-0.5)

            def postnorm(hb, l, gi, y_ap, y_units, need_sq):
                rstd_from_stat(hb)
                for kp in range(0, KC, 2):
                    ypair = y_ap(hb, kp)
                    yv = (ab_f if y_ap is y1_ap else qu_f)[:, hb, kp * HT:(kp + 2) * HT].rearrange("p (a t) -> p a t", a=2)
                    rb = rstd[:, hb, :].rearrange("p (a t) -> p a t", a=1).broadcast_to([128, 2, HT])
                    dve(lambda e, yv=yv, rb=rb: e.tensor_tensor(out=tmp[:, hb, :, :], in0=yv, in1=rb, op=ALU.mult),
                        y_units(hb, kp) + y_units(hb, kp + 1) + [("rstd", hb)], [("tmp", hb, 0), ("tmp", hb, 1)])
                    dve(lambda e, kp=kp: e.tensor_tensor(out=xs[:, hb, kp:kp + 2, :], in0=tmp[:, hb, :, :], in1=xs[:, hb, kp:kp + 2, :], op=ALU.add),
                        [("tmp", hb, 0), ("tmp", hb, 1), ("xs", hb, kp), ("xs", hb, kp + 1)], [("xs", hb, kp), ("xs", hb, kp + 1)])
                    if need_sq:
                        for k in (kp, kp + 1):
                            sq_from(hb, k, xs[:, hb, k, :], [("xs", hb, k)])

            def norm_sub(hb, affines):
                for k in range(KC):
                    stat_mm(hb, k)
                rstd_from_stat(hb)
                for k in range(KC):
                    r = k % 2
                    dve(lambda e, k=k, r=r: e.tensor_tensor(out=tmp[:, hb, r, :], in0=xs[:, hb, k, :], in1=rstd[:, hb, :], op=ALU.mult),
                        [("xs", hb, k), ("rstd", hb)], [("tmp", hb, r)])
                    for (Af, Bf, of, uf, xr) in affines:
                        act(of(k), tmp[:, hb, r, :], AF.Identity, [("tmp", hb, r)] + xr, [uf(k)], bias=Bf(k), scale=Af(k))

            def h_aff(l, ai, bcol):
                return (lambda k: dv[:, l, ai, k:k + 1], lambda k: modv[:, l, bcol + k: bcol + k + 1],
                        lambda k, hb=None: None, None, [("dv", l, ai), ("mod", l)])

            class Sub:
                def __init__(self, fn, rel=None, name="?"):
                    self.fn = fn
                    self.rel = rel
                    self.st = {}
                    self.name = name

            def dense_groups(st, hb, is_a, specs, rhs_fn, rhs_units_fn, epi):
                if is_a:
                    st["blks"] = [[ring.take(wd, r0, nkc, c0, 256) for (wd, r0, nkc, c0) in parts] for parts in specs["parts"]]
                pending = []
                if specs.get("kouter"):
                    grp = [(ci, oc, nb()) for ci in range(len(specs["parts"])) for oc in range(2)]
                    for kc in range(KC):
                        for ci, oc, bank in grp:
                            (wd, r0, nkc, c0) = specs["parts"][ci][0]
                            blk = st["blks"][ci][0]
                            mm(pb[bank][:, 0:HT], wring[:, blk.slot, kc, oc * 128:(oc + 1) * 128], rhs_fn(hb, r0 + kc),
                               kc == 0, kc == KC - 1, [("w", blk.slot), rhs_units_fn(hb, r0 + kc)], bank)
                    for ci, oc, bank in grp:
                        r = epi(specs["k0"] + ci * 2 + oc, bank)
                        if r is not None:
                            pending.append(r)
                    for p in pending:
                        p()
                    return
                for ci, parts in enumerate(specs["parts"]):
                    blks = st["blks"][ci]
                    for oc in range(2):
                        kglob = specs["k0"] + ci * 2 + oc
                        bank = nb()
                        total = sum(p[2] for p in parts)
                        cnt = 0
                        for (wd, r0, nkc, c0), blk in zip(parts, blks):
                            for kc in range(nkc):
                                mm(pb[bank][:, 0:HT], wring[:, blk.slot, kc, oc * 128:(oc + 1) * 128], rhs_fn(hb, r0 + kc),
                                   cnt == 0, cnt == total - 1, [("w", blk.slot), rhs_units_fn(hb, r0 + kc)], bank)
                                cnt += 1
                        r = epi(kglob, bank)
                        if r is not None:
                            pending.append(r)
                for p in pending:
                    p()

            def release(st):
                for lst in st.get("blks", []):
                    for blk in lst:
                        ring.done(blk)
                st["blks"] = []

            def build_pass(p):
                hidx_of = lambda hb: NH * p + hb
                subs = []
                l1 = do_l1 and p >= 1

                def f_xl(hb, is_a, st):
                    hidx = hidx_of(hb)
                    def xload(hx, s):
                        row0 = hx * HT + s * 128
                        xb = s % 2
                        S.add("sp", lambda e, xb=xb, row0=row0: e.dma_start(out=xin[xb][:], in_=x_d[row0:row0 + 128, :]),
                              reads=[], writes=[("xin", xb)], dma=True, chan=("xin", xb))
                    for s in range(NST):
                        xb = s % 2
                        if (hidx, s) not in xpref:
                            xload(hidx, s)
                        for g in range(2):
                            bank = nb()
                            for kk in range(4):
                                k = g * 4 + kk
                                S.add("pe", lambda e, xb=xb, k=k, kk=kk, bank=bank: e.transpose(out=pb[bank][:, kk * 128:(kk + 1) * 128],
                                                                                             in_=xin[xb][:, k * 128:(k + 1) * 128], identity=ident[:]),
                                      reads=[("xin", xb), "ident"], writes=[psu(bank)])
                            dst = xs[:, hb, g * 4:(g + 1) * 4, s * 128:(s + 1) * 128]
                            src = pb[bank][:, :].rearrange("p (a t) -> p a t", a=4)
                            wu = [psu(bank)] + [("xs", hb, g * 4 + kk) for kk in range(4)]
                            if g == 0:
                                S.add("act", lambda e, dst=dst, src=src: e.copy(out=dst, in_=src), reads=[psu(bank)], writes=wu)
                            else:
                                dve(lambda e, dst=dst, src=src: e.tensor_copy(out=dst, in_=src), [psu(bank)], wu)
                    if hidx + 1 < nhalf:
                        for s in range(2):
                            xload(hidx + 1, s)
                            xpref.add((hidx + 1, s))
                    for k in range(KC):
                        sq_from(hb, k, xs[:, hb, k, :], [("xs", hb, k)])
                    if hidx == 0:
                        act(sqpre[:].rearrange("p k t -> p (k t)"), xpre[:].rearrange("p k t -> p (k t)"), AF.Square, ["xpre"], ["sqpre"])
                        bank = nb()
                        for k in range(KC):
                            mm(pb[bank][:, 0:2], onesmean[:], sqpre[:, k, :], k == 0, k == KC - 1, ["onesmean", "sqpre"], bank)
                        act(rpre[:], pb[bank][:, 0:2], AF.Ln, [psu(bank)], [psu(bank), "rpre"], bias=EPS, scale=1.0)
                        act(rpre[:], rpre[:], AF.Exp, ["rpre"], ["rpre"], scale=-0.5)
                        for k in range(KC):
                            dve(lambda e, k=k: e.tensor_tensor(out=tpre[:, k, :], in0=xpre[:, k, :], in1=rpre[:], op=ALU.mult),
                                ["xpre", "rpre"], [("tpre", k)])
                            act(hpre[:, k, :], tpre[:, k, :], AF.Identity, [("tpre", k), ("dv", 0, 0), ("mod", 0)], [("hpre", k)],
                                bias=modv[:, 0, k:k + 1], scale=dv[:, 0, 0, k:k + 1])
                subs.append(Sub(f_xl, name="xl"))

                def f_n1(hb, is_a, st):
                    norm_sub(hb, [(lambda k: dv[:, 0, 0, k:k + 1], lambda k: modv[:, 0, k:k + 1],
                                   lambda k: hbuf[:, hb, k, :], lambda k: ("h", hb, k), [("dv", 0, 0), ("mod", 0)])])
                subs.append(Sub(f_n1, name="n1"))

                def mk_mx(pr):
                    def f(hb, is_a, st):
                        hidx = hidx_of(hb)
                        if is_a:
                            st["b"] = ring.take(cvi_d, 0, 8, pr * 256, 256)
                            st["c"] = ring.take(cvi_d, 0, 8, D + pr * 256, 256)
                            st["x"] = ring.take(cvi_d, 0, 8, 2 * D + pr * 256, 256)
                        pre = {}
                        kout = pr == 0 and hidx != 0
                        if kout:
                            for oc in range(2):
                                for nm in ("c", "x", "b"):
                                    pre[(oc, nm)] = nb()
                            for kc in range(KC):
                                for oc in range(2):
                                    for nm in ("c", "x", "b"):
                                        blk = st[nm]
                                        bank = pre[(oc, nm)]
                                        mm(pb[bank][:, 0:HT], wring[:, blk.slot, kc, oc * 128:(oc + 1) * 128], hbuf[:, hb, kc, :],
                                           kc == 0, kc == KC - 1, [("w", blk.slot), ("h", hb, kc)], bank)
                        for oc in range(2):
                            j = 2 * pr + oc
                            r = j % 2
                            banks = {}
                            for nm in ("c", "x", "b"):
                                if kout:
                                    banks[nm] = pre[(oc, nm)]
                                    continue
                                bank = nb()
                                banks[nm] = bank
                                blk = st[nm]
                                for kc in range(KC):
                                    mm(pb[bank][:, 0:HT], wring[:, blk.slot, kc, oc * 128:(oc + 1) * 128], hbuf[:, hb, kc, :],
                                       kc == 0, kc == KC - 1, [("w", blk.slot), ("h", hb, kc)], bank)
                            if hidx == 0:
                                bp = nb()
                                for gi, nm in enumerate(("c", "x")):
                                    blk = st[nm]
                                    for kc in range(KC):
                                        mm(pb[bp][:, 2 * gi:2 * gi + 2], wring[:, blk.slot, kc, oc * 128:(oc + 1) * 128], hpre[:, kc, :],
                                           kc == 0, kc == KC - 1, [("w", blk.slot), ("hpre", kc)], bp)
                                act(cpre[:], pb[bp][:, 0:2], AF.Copy, [psu(bp)], [psu(bp), "cpre"])
                                dve(lambda e, j=j, bp=bp: e.tensor_tensor(out=ztail[:, j, :], in0=cpre[:], in1=pb[bp][:, 2:4], op=ALU.mult),
                                    ["cpre", psu(bp)], [psu(bp), ("ztail", j)])
                            bc, bx, bbk = banks["c"], banks["x"], banks["b"]
                            act(csb[:, hb, r, :], pb[bc][:, 0:HT], AF.Copy, [psu(bc)], [psu(bc), ("csb", hb, r)])
                            dve(lambda e, j=j, r=r: e.tensor_copy(out=zb[:, hb, r, 0:2], in_=ztail[:, j, :]), [("ztail", j)], [("zb", hb, r)])
                            dve(lambda e, r=r, bx=bx: e.tensor_tensor(out=zb[:, hb, r, 2:HT + 2], in0=csb[:, hb, r, :], in1=pb[bx][:, 0:HT], op=ALU.mult),
                                [("csb", hb, r), psu(bx), ("zb", hb, r)], [psu(bx), ("zb", hb, r)])
                            if hidx == NHALO - 1:
                                dve(lambda e, j=j, r=r: e.tensor_scalar(out=ztail[:, j, :], in0=zb[:, hb, r, HT:HT + 2], scalar1=hv[:, 0:1], scalar2=None, op0=ALU.mult),
                                    [("zb", hb, r), "hv"], [("ztail", j)])
                            else:
                                dve(lambda e, j=j, r=r: e.tensor_copy(out=ztail[:, j, :], in_=zb[:, hb, r, HT:HT + 2]), [("zb", hb, r)], [("ztail", j)])
                            ck = lambda t, j=j: vecs[:, V_CK + t * 8 + j: V_CK + t * 8 + j + 1]
                            act(acc[:, hb, r, :], zb[:, hb, r, 2:HT + 2], AF.Identity, [("zb", hb, r), "vecs"], [("acc", hb, r)], scale=ck(2))
                            dve(lambda e, r=r, ck=ck: e.scalar_tensor_tensor(out=acc[:, hb, r, :], in0=zb[:, hb, r, 1:HT + 1], scalar=ck(1), in1=acc[:, hb, r, :],
                                                                             op0=ALU.mult, op1=ALU.add), [("zb", hb, r), ("acc", hb, r), "vecs"], [("acc", hb, r)])
                            dve(lambda e, r=r, ck=ck: e.scalar_tensor_tensor(out=acc[:, hb, r, :], in0=zb[:, hb, r, 0:HT], scalar=ck(0), in1=acc[:, hb, r, :],
                                                                             op0=ALU.mult, op1=ALU.add), [("zb", hb, r), ("acc", hb, r), "vecs"], [("acc", hb, r)])
                            dve(lambda e, j=j, r=r, bbk=bbk: e.tensor_tensor(out=u_ap(hb, j), in0=pb[bbk][:, 0:HT], in1=acc[:, hb, r, :], op=ALU.mult),
                                [psu(bbk), ("acc", hb, r)], [psu(bbk), ("u", hb, j)])

                    def rel(st):
                        for nm in ("b", "c", "x"):
                            ring.done(st[nm])
                    return Sub(f, rel, name="mx")
                for pr in range(4):
                    subs.append(mk_mx(pr))

                def mk_proj_post(wd, nkparts, cbks, rhs_fn, rhs_units_fn, y_ap, y_units, l, gi, last, need_sq, nm):
                    def f(hb, is_a, st):
                        parts = []
                        for cbk in cbks:
                            parts.append([(wd, r0, nkc, cbk * 256) for (r0, nkc) in nkparts])

                        def epi(k, bank):
                            act(y_ap(hb, k), pb[bank][:, 0:HT], AF.Identity, [psu(bank), ("dv", l, gi)], [psu(bank)] + y_units(hb, k),
                                scale=dv[:, l, gi, k:k + 1])
                            act(sq[:, hb, k, :], pb[bank][:, 0:HT], AF.Square, [psu(bank)], [psu(bank), ("sq", hb, k)])
                            return lambda: stat_mm(hb, k)
                        dense_groups(st, hb, is_a, {"parts": parts, "k0": 2 * cbks[0]}, rhs_fn, rhs_units_fn, epi)
                        if last:
                            postnorm(hb, l, gi, y_ap, y_units, need_sq)
                    return Sub(f, release, name=nm)

                u_rhs = lambda hb, kc: u_ap(hb, kc)
                u_units = lambda hb, kc: ("u", hb, kc)
                a_rhs = lambda hb, m: a_ap(hb, m)
                a_units = lambda hb, m: ("a", hb, m)
                h_rhs = lambda hb, kc: hbuf[:, hb, kc, :]
                h_units = lambda hb, kc: ("h", hb, kc)

                def mk_ffn1(l, pairs):
                    def f(hb, is_a, st):
                        if is_a:
                            st["blks"] = [[ring.take(ffi_d[l], 0, 8, pr * 256, 256), ring.take(ffi_d[l], 0, 8, DFF + pr * 256, 256)] for pr in pairs]
                        for pi, pr in enumerate(pairs):
                            gb, ub = st["blks"][pi]
                            pre = {}
                            if pr == 0:
                                for oc in range(2):
                                    pre[(oc, 0)] = nb()
                                    pre[(oc, 1)] = nb()
                                for kc in range(KC):
                                    for oc in range(2):
                                        for wi, wb_ in enumerate((gb, ub)):
                                            bank = pre[(oc, wi)]
                                            mm(pb[bank][:, 0:HT], wring[:, wb_.slot, kc, oc * 128:(oc + 1) * 128], hbuf[:, hb, kc, :],
                                               kc == 0, kc == KC - 1, [("w", wb_.slot), ("h", hb, kc)], bank)
                            for oc in range(2):
                                m = 2 * pr + oc
                                r = m % 2
                                if pr == 0:
                                    bg, bu = pre[(oc, 0)], pre[(oc, 1)]
                                else:
                                    bg = nb()
                                    for kc in range(KC):
                                        mm(pb[bg][:, 0:HT], wring[:, gb.slot, kc, oc * 128:(oc + 1) * 128], hbuf[:, hb, kc, :],
                                           kc == 0, kc == KC - 1, [("w", gb.slot), ("h", hb, kc)], bg)
                                    bu = nb()
                                    for kc in range(KC):
                                        mm(pb[bu][:, 0:HT], wring[:, ub.slot, kc, oc * 128:(oc + 1) * 128], hbuf[:, hb, kc, :],
                                           kc == 0, kc == KC - 1, [("w", ub.slot), ("h", hb, kc)], bu)
                                act(silu_t[:, hb, r, :], pb[bg][:, 0:HT], AF.Silu, [psu(bg)], [psu(bg), ("silu", hb, r)])
                                dve(lambda e, m=m, r=r, bu=bu: e.tensor_tensor(out=a_ap(hb, m), in0=silu_t[:, hb, r, :], in1=pb[bu][:, 0:HT], op=ALU.mult),
                                    [("silu", hb, r), psu(bu)], [psu(bu), ("a", hb, m)])
                    return Sub(f, release, name="f1_%d" % l)

                def ffn_subs(l, final_sq):
                    out = []
                    out.append(Sub(lambda hb, is_a, st: norm_sub(hb, [(lambda k: dv[:, l, 2, k:k + 1], lambda k: modv[:, l, 24 + k:25 + k],
                                                                       lambda k: hbuf[:, hb, k, :], lambda k: ("h", hb, k), [("dv", l, 2), ("mod", l)])]), name="n2_%d" % l))
                    for pairs in ([0, 1], [2, 3], [4, 5], [6, 7], [8, 9], [10]):
                        out.append(mk_ffn1(l, pairs))
                    kparts = [(0, 8), (8, 8), (16, 6)]
                    for cb in range(4):
                        out.append(mk_proj_post(ffo_d[l], kparts, [cb], a_rhs, a_units, y2_ap, y2_units, l, 3, cb == 3, final_sq, "f2_%d" % l))
                    return out

                subs.append(mk_proj_post(cvo_d, [(0, 8)], [0, 1], u_rhs, u_units, y1_ap, y1_units, 0, 1, False, True, "wo0"))
                subs.append(mk_proj_post(cvo_d, [(0, 8)], [2, 3], u_rhs, u_units, y1_ap, y1_units, 0, 1, True, True, "wo0"))
                subs.extend(ffn_subs(0, True))

                if p == 0:
                    def f_kvmod(hb, is_a, st):
                        if not is_a:
                            return
                        matvec(kvm_d, 2 * D, kvmod[:], vecs[:, V_KVB:V_KVB + 16], "kvmod")
                        dve(lambda e: e.scalar_tensor_tensor(out=akv[:], in0=kvmod[:, 8:16], scalar=1.0, in1=vecs[:, V_KVG:V_KVG + 8], op0=ALU.add, op1=ALU.mult),
                            ["kvmod", "vecs"], ["akv"])
                    subs.append(Sub(f_kvmod, name="kvmod"))
                if p == 1 and do_l1:
                    def mk_mod1(part):
                        def f_mod1(hb, is_a, st):
                            if not is_a:
                                return
                            matvec(modw_d[1], 2048, modv[:, 1, 16 * part:16 * part + 16], vecs[:, 48 + 16 * part:48 + 16 * part + 16],
                                   ("mod", 1), c0=2048 * part)
                            if part == 2:
                                derive_layer(1)
                        return Sub(f_mod1, name="mod1")
                    for part in range(3):
                        subs.append(mk_mod1(part))

                def f_nkv(hb, is_a, st):
                    affs = [(lambda k: akv[:, k:k + 1], lambda k: kvmod[:, k:k + 1], lambda k: u_ap(hb, k), lambda k: ("u", hb, k), ["akv", "kvmod"])]
                    if l1:
                        affs.append((lambda k: dv[:, 1, 0, k:k + 1], lambda k: modv[:, 1, k:k + 1], lambda k: hbuf[:, hb, k, :], lambda k: ("h", hb, k),
                                     [("dv", 1, 0), ("mod", 1)]))
                    norm_sub(hb, affs)
                subs.append(Sub(f_nkv, name="nkv"))

                def mk_kk(half_idx):
                    def f(hb, is_a, st):
                        hidx = hidx_of(hb)
                        slot4 = hidx % NKS
                        parts = [[(wkv_d, 0, 8, cbk * 256)] for cbk in (2 * half_idx, 2 * half_idx + 1)]

                        def epi(k, bank):
                            dve(lambda e: e.tensor_copy(out=Kb[:, k, slot4, :], in_=pb[bank][:, 0:HT]), [psu(bank)], [psu(bank), ("K", slot4, k)])
                            return None
                        dense_groups(st, hb, is_a, {"parts": parts, "k0": 4 * half_idx, "kouter": half_idx == 0}, u_rhs, u_units, epi)
                    return Sub(f, release, name="kk")
                subs.append(mk_kk(0))
                subs.append(mk_kk(1))

                def mk_vv(half_idx):
                    def f(hb, is_a, st):
                        hidx = hidx_of(hb)
                        slot4 = hidx % NKS
                        if is_a:
                            st["blks"] = [[ring.take(wkv_d, 0, 8, D + cbk * 256, 256)] for cbk in (2 * half_idx, 2 * half_idx + 1)]
                        for ci, cbk in enumerate((2 * half_idx, 2 * half_idx + 1)):
                            blk = st["blks"][ci][0]
                            for s in range(NST):
                                bank = nb()
                                for kc in range(KC):
                                    mm(pb[bank][:, 0:256], u_ap(hb, kc, cols=slice(s * 128, (s + 1) * 128)), wring[:, blk.slot, kc, :],
                                       kc == 0, kc == KC - 1, [("w", blk.slot), ("u", hb, kc)], bank)
                                vt = slot4 * NST + s
                                if hidx < NHALO:
                                    S.add("act", lambda e, vt=vt, cbk=cbk, bank=bank: e.activation(out=Vb[:, vt, cbk * 256:(cbk + 1) * 256], in_=pb[bank][:, 0:256],
                                                                                                   func=AF.Identity, scale=hv[:, 0:1]),
                                          reads=[psu(bank), "hv"], writes=[psu(bank), ("V", vt, cbk)])
                                else:
                                    act(Vb[:, vt, cbk * 256:(cbk + 1) * 256], pb[bank][:, 0:256], AF.Copy, [psu(bank)], [psu(bank), ("V", vt, cbk)])
                    return Sub(f, release, name="vv")
                subs.append(mk_vv(0))
                subs.append(mk_vv(1))

                if l1:
                    def mk_qq(half_idx):
                        def f(hb, is_a, st):
                            parts = [[(wq_d, 0, 8, cbk * 256)] for cbk in (2 * half_idx, 2 * half_idx + 1)]

                            def epi(k, bank):
                                S.add("act", lambda e: e.mul(out=q_ap(hb, k), in_=pb[bank][:, 0:HT], mul=0.125), reads=[psu(bank)], writes=[psu(bank), ("q", hb, k)])
                                return None
                            dense_groups(st, hb, is_a, {"parts": parts, "k0": 4 * half_idx}, h_rhs, h_units, epi)
                        return Sub(f, release, name="qq")
                    subs.append(mk_qq(0))
                    subs.append(mk_qq(1))

                    def mk_at(hp):
                        def f(hb, is_a, st):
                            hidx = hidx_of(hb)

                            def key_loc(qi, kt):
                                ek = hidx * HT + qi * 128 - 512 + 128 * kt
                                hk = ek // HT
                                off = ek % HT
                                return hk, off, ek

                            for qh in range(NST // 2):
                                items = [(2 * qh + qq, hh) for qq in range(2) for hh in range(2)]
                                osb = 4 + ((hp * (NST // 2) + qh) % 2)

                                def scores(n):
                                    qi, hh = items[n]
                                    h = 2 * hp + hh
                                    rows = slice(64 * hh, 64 * hh + 64)
                                    b0 = 2 * (n % 2)
                                    b1 = b0 + 1
                                    for kt in (1, 2, 0, 3, 4):
                                        hk, off, ek = key_loc(qi, kt)
                                        s4 = hk % NKS
                                        bank, c0 = (b0, kt * 128) if kt < 3 else (b1, (kt - 3) * 128)
                                        outp = pb[bank][:, c0:c0 + 128]
                                        st_flag = kt != 4
                                        S.add("pe", lambda e, outp=outp, s4=s4, off=off, st_flag=st_flag, kt=kt, rows=rows, qi=qi: e.matmul(
                                            out=outp, lhsT=Kb[rows, hp, s4, off:off + 128],
                                            rhs=q_ap(hb, hp, rows=rows, cols=slice(qi * 128, qi * 128 + 128)),
                                            start=st_flag, stop=(kt in (1, 2)), skip_group_check=True),
                                            reads=[("K", s4, hp), ("q", hb, hp)], writes=[psu(bank)])
                                    S.add("pe", lambda e, b0=b0: e.matmul(out=pb[b0][:, 0:128], lhsT=identb[:], rhs=mask0[:], start=False, stop=True, skip_group_check=True),
                                          reads=["identb", "mask0"], writes=[psu(b0)])
                                    S.add("pe", lambda e, b1=b1, h=h: e.matmul(out=pb[b1][:, 0:256], lhsT=identb[:], rhs=bhi[:, h, :, :].rearrange("p a j -> p (a j)"),
                                                                             start=False, stop=False, skip_group_check=True), reads=["identb", "bhi"], writes=[psu(b1)])
                                    S.add("pe", lambda e, b1=b1, h=h: e.matmul(out=pb[b1][:, 0:256], lhsT=identb[:], rhs=blo[:, h, :, :].rearrange("p a j -> p (a j)"),
                                                                             start=False, stop=True, skip_group_check=True), reads=["identb", "blo"], writes=[psu(b1)])
                                    pi = n % 2
                                    act(Pt[:, pi, 0:384], pb[b0][:, 0:384], AF.Exp, [psu(b0), "cbs"], [psu(b0), ("P", pi)], bias=cbs[:, h:h + 1], scale=1.0)
                                    act(Pt[:, pi, 384:640], pb[b1][:, 0:256], AF.Exp, [psu(b1)], [psu(b1), ("P", pi)])

                                def pv(n):
                                    qi, hh = items[n]
                                    qq = qi % 2
                                    h = 2 * hp + hh
                                    rows = slice(64 * hh, 64 * hh + 64)
                                    pi = n % 2
                                    for kt in range(5):
                                        hk, off, ek = key_loc(qi, kt)
                                        vt = (hk % NKS) * NST + off // 128
                                        mm(pb[osb][rows, qq * 128:(qq + 1) * 128], Vb[:, vt, h * 64:(h + 1) * 64], Pt[:, pi, kt * 128:(kt + 1) * 128],
                                           kt == 0, kt == 4, [("V", vt, h // 4), ("P", pi)], osb)
                                    for kt in range(5):
                                        hk, off, ek = key_loc(qi, kt)
                                        onesl = hv_bf if ek < HALO else ones_bf
                                        mm(pb[osb][rows, 256 + qq * 128:256 + (qq + 1) * 128], onesl[:], Pt[:, pi, kt * 128:(kt + 1) * 128],
                                           kt == 0, kt == 4, ["hv_bf", "ones_bf", ("P", pi)], osb)

                                scores(0)
                                for n in range(1, len(items)):
                                    scores(n)
                                    pv(n - 1)
                                pv(len(items) - 1)
                                cs = slice(qh * 256, qh * 256 + 256)
                                dve(lambda e, osb=osb, cs=cs: e.reciprocal(out=rc[:, hb, cs], in_=pb[osb][:, 256:512]), [psu(osb)], [psu(osb), ("rc", hb, qh)])
                                dve(lambda e, osb=osb, cs=cs: e.tensor_tensor(out=u_ap(hb, hp, cols=cs), in0=pb[osb][:, 0:256], in1=rc[:, hb, cs], op=ALU.mult),
                                    [psu(osb), ("rc", hb, qh)], [psu(osb), ("u", hb, hp)])
                        return Sub(f, name="at")
                    for hp in range(8):
                        subs.append(mk_at(hp))

                    subs.append(mk_proj_post(wo_d, [(0, 8)], [0, 1], u_rhs, u_units, y1_ap, y1_units, 1, 1, False, True, "wo1"))
                    subs.append(mk_proj_post(wo_d, [(0, 8)], [2, 3], u_rhs, u_units, y1_ap, y1_units, 1, 1, True, True, "wo1"))
                    subs.extend(ffn_subs(1, False))

                if p >= 1:
                    def f_out(hb, is_a, st):
                        hidx = hidx_of(hb)
                        for s in range(NST):
                            for g in range(2):
                                bank = nb()
                                for kk in range(4):
                                    k = g * 4 + kk
                                    S.add("pe", lambda e, k=k, kk=kk, s=s, bank=bank: e.transpose(out=pb[bank][:, kk * 128:(kk + 1) * 128],
                                                                                                   in_=xs[:, hb, k, s * 128:(s + 1) * 128], identity=ident[:]),
                                          reads=[("xs", hb, k), "ident"], writes=[psu(bank)])
                                if g == 0:
                                    act(xout_ap(hb, s)[:, 0:512], pb[bank][:, :], AF.Copy, [psu(bank)], [psu(bank)] + xout_units(hb, s)[0:len(xout_units(hb, s)) // 2])
                                else:
                                    dve(lambda e, s=s, bank=bank: e.tensor_copy(out=xout_ap(hb, s)[:, 512:1024], in_=pb[bank][:, :]),
                                        [psu(bank)], [psu(bank)] + xout_units(hb, s)[len(xout_units(hb, s)) // 2:])
                            row0 = (hidx - NHALO) * HT + s * 128
                            S.add("sp", lambda e, s=s, row0=row0: e.dma_start(out=out_d[row0:row0 + 128, :], in_=xout_ap(hb, s)),
                                  reads=xout_units(hb, s), writes=[("out", row0)], dma=True, chan=("xout", hb, s))
                    subs.append(Sub(f_out, name="out"))
                return subs

            items = []
            for p in range(nhalf // NH):
                items.extend(build_pass(p))
            if skew and NH == 2:
                for idx in range(len(items) + 1):
                    if idx < len(items):
                        it = items[idx]
                        S.label = it.name + ":A"
                        it.fn(0, True, it.st)
                    if idx >= 1:
                        it = items[idx - 1]
                        S.label = it.name + ":B"
                        it.fn(1, False, it.st)
                        if it.rel is not None:
                            it.rel(it.st)
            else:
                for it in items:
                    for hb in range(NH):
                        S.label = it.name + (":A" if hb == 0 else ":B")
                        it.fn(hb, hb == 0, it.st)
                    if it.rel is not None:
                        it.rel(it.st)
            outs = [("out", r) for r in range(0, (nhalf - NHALO) * HT, 128)]
            S.add("sp", lambda e: e.nop(), reads=outs, writes=[])

        S.dry = True
        rec = WRing(S, wring, None)
        emit_program(rec)
        plan = rec.plan
        S.dry = False
        ring = WRing(S, wring, plan)
        emit_program(ring)
        assert ring.i_use == len(plan), (ring.i_use, len(plan))

        cnt = S.analyze()
        print("ops:", len(S.ops), "sems:", len(S.sem_keys), "max sem val:", max(cnt.values()))
        sems = {}
        for i, key in enumerate(S.sem_keys):
            sems[key] = es.enter_context(nc.semaphore(f"s{i}"))
        block = es.enter_context(nc.Block())
        S.emit(block, sems)
    nc._sched_ops = S.ops
    return nc


def _fm(v):
    v = np.asarray(v, dtype=np.float32).reshape(-1, KC, 128)
    return np.ascontiguousarray(v.transpose(2, 0, 1).reshape(128, -1))


def prepare_inputs(x, c, mod_w, mod_b, norm_g, ffn_w_in, ffn_w_out, conv_w_in, conv_k, conv_w_out, kv_mod_w, kv_mod_b,
                   kv_norm_g, w_kv, attn_w_q, attn_w_o, rel_bias):
    f = lambda a: np.ascontiguousarray(np.asarray(a, dtype=np.float32))
    x = f(x)
    B, Sq, _ = x.shape
    rb = f(rel_bias)[0]
    i = np.arange(128)[:, None]
    j = np.arange(128)[None, :]
    idx0 = np.clip(j - i, -128, 128) + 128
    idx1 = np.clip(128 + j - i, -128, 128) + 128
    bt = np.stack([rb[:, idx1], rb[:, idx0]], axis=1)
    bt = np.ascontiguousarray(bt.transpose(2, 0, 1, 3).reshape(128, 4096))
    cbv = np.ascontiguousarray(np.broadcast_to(rb[:, 256][None, :], (128, 16))).astype(np.float32)
    ident = np.eye(128, dtype=np.float32)
    shared = dict(
        cb=cbv, btiles=bt, ident=ident,
        mod_w0=f(mod_w[0]), mod_w1=f(mod_w[1]), ffn_in0=f(ffn_w_in[0]), ffn_in1=f(ffn_w_in[1]),
        ffn_out0=f(ffn_w_out[0]), ffn_out1=f(ffn_w_out[1]), conv_in=f(conv_w_in[0]), conv_out=f(conv_w_out[0]),
        kv_mod_w=f(kv_mod_w), w_kv=f(w_kv), w_q=f(attn_w_q[0]), w_o=f(attn_w_o[0]),
    )
    vec_common = np.concatenate([
        _fm(f(mod_b)[0].reshape(6, D)), _fm(f(mod_b)[1].reshape(6, D)),
        _fm(f(norm_g).reshape(8, D)), _fm(f(conv_k)[0]), _fm(f(kv_mod_b).reshape(2, D)), _fm(f(kv_norm_g).reshape(1, D)),
    ], axis=1)
    in_maps = []
    per_b = N_CORES // B
    for core in range(N_CORES):
        b = core // per_b
        qd = core % per_b
        t0 = qd * OWN
        xc = np.zeros((HALO + OWN, D), np.float32)
        xp = np.zeros((2, D), np.float32)
        if qd > 0:
            xc[:] = x[b, t0 - HALO:t0 + OWN]
            xp[:] = x[b, t0 - HALO - 2:t0 - HALO]
        else:
            xc[HALO:] = x[b, 0:OWN]
        xpre = np.ascontiguousarray(xp.reshape(2, KC, 128).transpose(2, 1, 0).reshape(128, 16))
        vecs = np.ascontiguousarray(np.concatenate([vec_common, _fm(f(c)[b].reshape(1, D))], axis=1))
        assert vecs.shape == (128, NV)
        hvv = np.full((128, 1), 1.0 if qd > 0 else 0.0, np.float32)
        m = dict(shared)
        m.update(x=xc, xpre=xpre, vecs=vecs, hv=hvv)
        in_maps.append(m)
    return in_maps


_NC_CACHE = {}


def kernel(x, c, mod_w, mod_b, norm_g, ffn_w_in, ffn_w_out, conv_w_in, conv_k, conv_w_out, kv_mod_w, kv_mod_b,
           kv_norm_g, w_kv, attn_w_q, attn_w_o, rel_bias):
    in_maps = prepare_inputs(x, c, mod_w, mod_b, norm_g, ffn_w_in, ffn_w_out, conv_w_in, conv_k, conv_w_out, kv_mod_w,
                             kv_mod_b, kv_norm_g, w_kv, attn_w_q, attn_w_o, rel_bias)
    if "nc" not in _NC_CACHE:
        _NC_CACHE["nc"] = build_nc(skew=False)
    nc = _NC_CACHE["nc"]
    res = run_bass_kernel_spmd(nc, in_maps, core_ids=list(range(N_CORES)))
    B, Sq, _ = np.asarray(x).shape
    out = np.empty((B, Sq, D), np.float32)
    per_b = N_CORES // B
    for core in range(N_CORES):
        b = core // per_b
        qd = core % per_b
        out[b, qd * OWN:(qd + 1) * OWN] = np.asarray(res.results[core]["out"], dtype=np.float32)
    return out
```
